# Optimizing a Trainium2 kernel written in Bass

```python
import jax, jax.numpy as jnp
from jax import lax
import numpy as np

D_MODEL = 2048
BATCH = 4
SEQ = 4096
DEPTH = 2

CHUNK = 64
GLA_HEADS = 4
GLA_V = D_MODEL // 2
GLA_DV = GLA_V // GLA_HEADS
GLA_DK = GLA_DV // 2
GLA_QK = GLA_HEADS * GLA_DK
GLA_LOWRANK = 16
GLA_TAU = 16.0
CONV_WIDTH = D_MODEL // 2
CONV_K = 3
D_FF = 11 * D_MODEL // 4
EPS = 1e-6
SPLITS = (GLA_QK, GLA_QK, GLA_V, GLA_V, GLA_LOWRANK,
          CONV_WIDTH, CONV_WIDTH, CONV_WIDTH, D_MODEL, D_MODEL)
N_IN = sum(SPLITS)

kernel_name = "hybrid_gla_shortconv_convffn_adaln"


def rmsnorm(x, g):
    xf = x.astype(jnp.float32)
    y = xf * lax.rsqrt(jnp.mean(xf * xf, axis=-1, keepdims=True) + EPS)
    return (y * g.astype(jnp.float32)).astype(x.dtype)


def causal_dwconv(x, w):
    k = w.shape[0]
    s = x.shape[1]
    xp = jnp.pad(x, ((0, 0), (k - 1, 0), (0, 0)))
    y = w[0] * xp[:, 0:s]
    for i in range(1, k):
        y = y + w[i] * xp[:, i:i + s]
    return y


def gla_chunked(q, k, v, log_a):
    b, s, h, dk = q.shape
    dv = v.shape[-1]
    nc = s // CHUNK

    def to_chunks(t):
        return t.reshape(b, nc, CHUNK, h, t.shape[-1]).transpose(1, 0, 3, 2, 4)

    def step(state, inp):
        q_c, k_c, v_c, a_c = inp
        cum = jnp.cumsum(a_c, axis=2)
        cum_end = cum[:, :, -1:, :]
        k_dec = k_c * jnp.exp(cum_end - cum)
        state = (jnp.exp(cum_end[:, :, 0, :])[..., None] * state
                 + jnp.einsum('bhld,bhle->bhde', k_dec, v_c))
        o_c = jnp.einsum('bhld,bhde->bhle', q_c, state)
        return state, o_c

    s0 = jnp.zeros((b, h, dk, dv), jnp.float32)
    _, o = lax.scan(step, s0, (to_chunks(q), to_chunks(k), to_chunks(v), to_chunks(log_a)))
    return o.transpose(1, 0, 3, 2, 4).reshape(b, s, h, dv)


def token_mixer(h, w_in, w_a2, b_a2, gla_norm_g, w_out_gla, conv_mix_w, w_out_conv, w_o):
    bsz, s, _ = h.shape
    split_idx = [int(i) for i in np.cumsum(SPLITS)[:-1]]
    q, k, v, r, lr, cb, cc, cx, ga, gb = jnp.split(h @ w_in, split_idx, axis=-1)

    log_a = jax.nn.log_sigmoid((lr @ w_a2 + b_a2).astype(jnp.float32)) / GLA_TAU
    heads = lambda t, d: t.astype(jnp.float32).reshape(bsz, s, GLA_HEADS, d)
    o = gla_chunked(heads(q, GLA_DK) * (GLA_DK ** -0.5), heads(k, GLA_DK),
                    heads(v, GLA_DV), heads(log_a, GLA_DK))
    o = rmsnorm(o, gla_norm_g).astype(h.dtype).reshape(bsz, s, GLA_V)
    y_a = (o * jax.nn.silu(r)) @ w_out_gla

    y_b = (cb * causal_dwconv(cc * cx, conv_mix_w)) @ w_out_conv

    m = jax.nn.sigmoid(ga) * y_a + jax.nn.sigmoid(gb) * y_b
    return m @ w_o


def channel_mixer(h, w_up, ffn_conv_w, w_down):
    gate, up = jnp.split(h @ w_up, 2, axis=-1)
    return (jax.nn.gelu(causal_dwconv(gate, ffn_conv_w)) * up) @ w_down


def setup_inputs(seed: int = 0) -> dict:
    key = jax.random.key(seed)
    ks = jax.random.split(key, 16)
    nrm = lambda k, shape, scale: jax.random.normal(k, shape, jnp.float32) * scale
    return {
        "x": nrm(ks[0], (BATCH, SEQ, D_MODEL), 1.0),
        "c": nrm(ks[1], (BATCH, D_MODEL), 1.0),
        "w_ada": nrm(ks[2], (DEPTH, D_MODEL, 6 * D_MODEL), 0.5 * D_MODEL ** -0.5),
        "b_ada": nrm(ks[3], (DEPTH, 6 * D_MODEL), 0.01),
        "norm_g": 1.0 + nrm(ks[4], (DEPTH, 4, D_MODEL), 0.05),
        "w_in": nrm(ks[5], (DEPTH, D_MODEL, N_IN), D_MODEL ** -0.5),
        "w_a2": nrm(ks[6], (DEPTH, GLA_LOWRANK, GLA_QK), GLA_LOWRANK ** -0.5),
        "b_a2": nrm(ks[7], (DEPTH, GLA_QK), 0.1),
        "gla_norm_g": 1.0 + nrm(ks[8], (DEPTH, GLA_DV), 0.05),
        "w_out_gla": nrm(ks[9], (DEPTH, GLA_V, D_MODEL), GLA_V ** -0.5),
        "conv_mix_w": nrm(ks[10], (DEPTH, CONV_K, CONV_WIDTH), 0.5),
        "w_out_conv": nrm(ks[11], (DEPTH, CONV_WIDTH, D_MODEL), CONV_WIDTH ** -0.5),
        "w_o": nrm(ks[12], (DEPTH, D_MODEL, D_MODEL), D_MODEL ** -0.5),
        "w_up": nrm(ks[13], (DEPTH, D_MODEL, 2 * D_FF), D_MODEL ** -0.5),
        "ffn_conv_w": nrm(ks[14], (DEPTH, CONV_K, D_FF), 0.5),
        "w_down": nrm(ks[15], (DEPTH, D_FF, D_MODEL), D_FF ** -0.5),
    }


def reference(x, c, w_ada, b_ada, norm_g, w_in, w_a2, b_a2, gla_norm_g, w_out_gla,
              conv_mix_w, w_out_conv, w_o, w_up, ffn_conv_w, w_down):
    for l in range(DEPTH):
        mod = jax.nn.silu(c) @ w_ada[l] + b_ada[l]
        sh1, sc1, g1, sh2, sc2, g2 = [t[:, None, :] for t in jnp.split(mod, 6, axis=-1)]
        h = rmsnorm(x, norm_g[l, 0]) * (1.0 + sc1) + sh1
        y = token_mixer(h, w_in[l], w_a2[l], b_a2[l], gla_norm_g[l], w_out_gla[l],
                        conv_mix_w[l], w_out_conv[l], w_o[l])
        x = x + g1 * rmsnorm(y, norm_g[l, 1])
        h = rmsnorm(x, norm_g[l, 2]) * (1.0 + sc2) + sh2
        y = channel_mixer(h, w_up[l], ffn_conv_w[l], w_down[l])
        x = x + g2 * rmsnorm(y, norm_g[l, 3])
    return x
```

```python
import numpy as np
from contextlib import ExitStack
import concourse.bass as bass
import concourse.mybir as mybir
from concourse.bass_utils import run_bass_kernel_spmd

F32 = mybir.dt.float32
BF16 = mybir.dt.bfloat16
AF = mybir.ActivationFunctionType
ALU = mybir.AluOpType

D = 2048
KC = 16
T = 512
NTILE = 8
DFF = 5632
FC = 44
EPS = 1e-6
NSLOT = 6
SLOTW = 2048

PQ, PK, PV, PR, PCB, PCC, PCX, PGA, PGB = 0, 4, 8, 16, 24, 32, 40, 48, 64


class Op:
    __slots__ = ("eng", "fns", "deps", "signal", "val", "dma_sem", "dma_val")

    def __init__(self, eng, fns, deps):
        self.eng = eng
        self.fns = fns
        self.deps = deps
        self.signal = False
        self.val = None
        self.dma_sem = None
        self.dma_val = None


class Buf:
    __slots__ = ("w", "r", "lo", "hi", "psum", "last")

    def __init__(self, lo=None, hi=None, psum=False):
        self.w = None
        self.r = []
        self.lo = lo
        self.hi = hi
        self.psum = psum
        self.last = None


class Prog:
    def __init__(self):
        self.ops = {"pe": [], "act": [], "dve": [], "pool": [], "sp": []}
        self.arena_bufs = []
        self.dma_counts = {}

    def abuf(self, lo, hi):
        b = Buf(lo, hi)
        self.arena_bufs.append(b)
        return b

    def _deps(self, eng, reads, writes):
        deps = []
        for b in reads:
            if b.psum:
                if b.last is not None and b.last.eng != eng:
                    deps.append(b.last)
            elif b.w is not None:
                deps.append(b.w)
        for b in writes:
            if b.psum:
                if b.last is not None and b.last.eng != eng:
                    deps.append(b.last)
                continue
            if b.w is not None:
                deps.append(b.w)
            deps.extend(b.r)
            if b.lo is not None:
                for o in self.arena_bufs:
                    if o is not b and o.lo < b.hi and b.lo < o.hi:
                        if o.w is not None:
                            deps.append(o.w)
                        deps.extend(o.r)
        return deps

    def _note(self, op, reads, writes):
        for b in reads:
            if b.psum:
                b.last = op
            else:
                b.r.append(op)
        for b in writes:
            if b.psum:
                b.last = op
                continue
            b.w = op
            b.r = []
            if b.lo is not None:
                for o in self.arena_bufs:
                    if o is not b and o.lo < b.hi and b.lo < o.hi:
                        o.w = None
                        o.r = []

    def op(self, eng, fns, reads=(), writes=(), deps=()):
        if not isinstance(fns, (list, tuple)):
            fns = [fns]
        d = self._deps(eng, reads, writes) + [x for x in deps if x is not None]
        o = Op(eng, list(fns), d)
        self.ops[eng].append(o)
        self._note(o, reads, writes)
        return o

    def dma(self, eng, fn, sem, reads=(), writes=(), deps=()):
        o = self.op(eng, fn, reads, writes, deps)
        c = self.dma_counts.get(id(sem), 0) + 16
        self.dma_counts[id(sem)] = c
        o.dma_sem = sem
        o.dma_val = c
        return o

    def finalize(self, engsem):
        for lst in self.ops.values():
            for o in lst:
                for dd in o.deps:
                    if dd.dma_sem is None:
                        dd.signal = True
        for eng, lst in self.ops.items():
            c = 0
            for o in lst:
                if o.signal:
                    c += 1
                    o.val = c
        self.engsem = engsem

    def replay(self, eng, e):
        seen = {}
        for o in self.ops[eng]:
            for dd in o.deps:
                if dd.dma_sem is not None:
                    sem, val = dd.dma_sem, dd.dma_val
                else:
                    if dd.eng == eng and False:
                        continue
                    sem, val = self.engsem[dd.eng], dd.val
                k = id(sem)
                if seen.get(k, 0) >= val:
                    continue
                seen[k] = val
                e.wait_ge(sem, val)
            ins = None
            for f in o.fns:
                ins = f(e)
            if o.dma_sem is not None:
                ins.then_inc(o.dma_sem, 16)
            elif o.signal:
                ins.then_inc(self.engsem[eng], 1)


def build(n_steps=1, debug=False):
    nc = bass.Bass("TRN2", target_bir_lowering=False)

    def din(name, shape):
        return nc.dram_tensor(name, list(shape), F32, kind="ExternalInput").ap()

    def dout(name, shape):
        return nc.dram_tensor(name, list(shape), F32, kind="ExternalOutput").ap()

    xin = din("xin", [n_steps, 128, KC * T])
    cvec = din("cvec", [128, KC])
    wada = din("wada", [96, 128, 2048])
    bada = din("bada", [128, 96])
    ngd = din("ng", [128, 64])
    win = din("win", [80, 128, 2048])
    wlr = din("wlr", [128, 256])
    wa2e = din("wa2e", [17, 512])
    glag = din("glag", [128, 2])
    woutg = din("woutg", [16, 128, 1024])
    woutc = din("woutc", [16, 128, 1024])
    cmw = din("cmw", [128, 24])
    wo = din("wo", [16, 128, 2048])
    wup = din("wup", [88, 128, 2048])
    fcw = din("fcw", [128, 132])
    wdn = din("wdn", [64, 128, 1408])
    masku = din("masku", [128, 128])
    cind = din("cind", [128, 2])
    s_in = din("s_in", [128, 1024])
    uh_in = din("uh_in", [128, 16])
    gh_in = din("gh_in", [128, 88])
    xout = dout("xout", [n_steps, 128, KC * T])
    s_out = dout("s_out", [128, 1024])
    uh_out = dout("uh_out", [128, 16])
    gh_out = dout("gh_out", [128, 88])
    dbg_out = {}
    if debug:
        for nm, w in [("d_h", KC * T // 2), ("d_la", 2048), ("d_E", 2048), ("d_dec", 32), ("d_q", 1024),
                      ("d_k", 1024), ("d_v", 2048), ("d_o", 4096), ("d_gin", 2048), ("d_cvb", 2048),
                      ("d_m", 4096), ("d_xmid", KC * T), ("d_mod", 96)]:
            dbg_out[nm] = dout(nm, [128, w])

    P = Prog()
    with ExitStack() as es:
        E = es.enter_context

        def sb(name, shape, dt):
            return E(nc.sbuf_tensor(name, list(shape), dt))

        xres = sb("xres", [128, KC * T], F32)
        hbuf = sb("hbuf", [128, KC * T], BF16)
        arena = sb("arena", [128, 40960], BF16)
        ring = sb("ring", [128, NSLOT * SLOTW], BF16)
        sq = sb("sq", [128, 2 * T], BF16)
        rstd = sb("rstd", [128, 2 * T], F32)
        tmpf = sb("tmpf", [128, 4 * T], F32)
        ubuf = sb("ubuf", [128, 2 * (T + 2)], F32)
        cbuf = sb("cbuf", [128, 2 * T], F32)
        S32 = sb("S32", [128, 1024], F32)
        Sb = sb("Sb", [128, 2048], BF16)
        lrT = sb("lrT", [32, T], BF16)
        onesD = sb("onesD", [128, 128], BF16)
        ones256 = sb("ones256", [128, 128], BF16)
        maskU = sb("maskU", [128, 128], F32)
        cindt = sb("cindt", [128, 2], F32)
        wa2t = sb("wa2t", [32, 512], BF16)
        wlrt = sb("wlrt", [128, 256], BF16)
        modv = sb("modv", [128, 96], F32)
        badat = sb("badat", [128, 96], F32)
        ngt = sb("ngt", [128, 64], F32)
        coef = sb("coef", [128, 96], F32)
        glagt = sb("glagt", [128, 2], F32)
        cmwt = sb("cmwt", [128, 24], F32)
        fcwt = sb("fcwt", [128, 132], F32)
        cst = sb("cst", [128, KC], F32)
        csb = sb("csb", [128, KC], BF16)
        decay = sb("decay", [128, 32], F32)
        uhalo = sb("uhalo", [128, 16], F32)
        ghalo = sb("ghalo", [128, 88], F32)
        epst = sb("epst", [128, 1], F32)

        banks = [E(nc.psum_tensor(f"bank{i}", [128, 512], F32)) for i in range(8)]
        bankb = [Buf(psum=True) for _ in range(8)]

        engsem = {k: E(nc.semaphore(f"sem_{k}")) for k in ["pe", "act", "dve", "pool", "sp"]}
        slot_sem = [E(nc.semaphore(f"slot{i}")) for i in range(NSLOT)]
        sem_x = E(nc.semaphore("sem_x"))
        sem_c = E(nc.semaphore("sem_c"))
        sem_o = E(nc.semaphore("sem_o"))

        A0, B0 = 0, 16384

        def aview(lo, n, dt=BF16):
            ap = arena[:, lo:lo + n]
            return ap.bitcast(F32) if dt == F32 else ap

        qT = aview(A0 + 0, 2048)
        kdec = aview(A0 + 2048, 2048)
        vtm = aview(A0 + 4096, 4096)
        Ef = aview(A0 + 8192, 4096, F32)
        laf = aview(A0 + 12288, 4096, F32)
        oT = aview(A0 + 8192, 8192, F32)
        ybuf = aview(A0, 16384, F32)
        rs = aview(B0 + 0, 4096)
        gin = aview(B0 + 4096, 4096)
        cvb = aview(B0 + 8192, 4096)
        mbuf = aview(B0 + 12288, 8192)
        actb = aview(B0, FC * T)

        def ab(lo, n):
            return P.abuf(lo, lo + n)

        b_q = [ab(A0 + h * 512, 512) for h in range(4)]
        b_kdec = ab(A0 + 2048, 2048)
        b_v = ab(A0 + 4096, 4096)
        b_E = ab(A0 + 8192, 4096)
        b_la = [ab(A0 + 12288 + tb * 1024, 1024) for tb in range(4)]
        b_o = [ab(A0 + 8192 + c * 1024, 1024) for c in range(8)]
        b_y = [ab(A0 + c * 1024, 1024) for c in range(16)]
        b_rs = [ab(B0 + c * 512, 512) for c in range(8)]
        b_gin = [ab(B0 + 4096 + c * 512, 512) for c in range(8)]
        b_cvb = [ab(B0 + 8192 + c * 512, 512) for c in range(8)]
        b_m = [ab(B0 + 12288 + c * 512, 512) for c in range(16)]
        b_act = [ab(B0 + c * 512, 512) for c in range(FC)]

        b_x = [Buf() for _ in range(KC)]
        b_h = [Buf() for _ in range(KC)]
        b_slot = [Buf() for _ in range(NSLOT)]
        b_sq = [Buf(), Buf()]
        b_rstd = [Buf(), Buf()]
        b_tmp = [Buf() for _ in range(4)]
        b_ubuf = [Buf(), Buf()]
        b_cbuf = [Buf(), Buf()]
        b_S = [Buf() for _ in range(4)]
        b_Sb = [[Buf(), Buf()] for _ in range(4)]
        b_lrT = Buf()
        b_const = Buf()
        b_mod = Buf()
        b_coef = Buf()
        b_decay = Buf()
        b_uh = [Buf() for _ in range(8)]
        b_gh = [Buf() for _ in range(FC)]

        def xc(c):
            return xres[:, c * T:(c + 1) * T]

        def hc(c):
            return hbuf[:, c * T:(c + 1) * T]

        def slot(i, n=SLOTW):
            return ring[:, i * SLOTW:i * SLOTW + n]

        piece_ctr = [0]

        def load_piece(src_ap, ncols):
            i = piece_ctr[0]
            piece_ctr[0] += 1
            s = i % NSLOT
            P.dma("pool", lambda e, s=s, src_ap=src_ap, ncols=ncols: e.dma_start(out=slot(s, ncols), in_=src_ap),
                  slot_sem[s], writes=[b_slot[s]])
            return s

        def cdma(dst, src, eng="sp", sem=None):
            return P.dma(eng, lambda e: e.dma_start(out=dst, in_=src), sem or sem_c, writes=[b_const])

        cdma(cst[:], cvec)
        cdma(badat[:], bada)
        cdma(ngt[:], ngd)
        cdma(glagt[:], glag)
        cdma(cmwt[:], cmw)
        cdma(fcwt[:], fcw)
        cdma(maskU[:], masku)
        cdma(cindt[:], cind)
        P.dma("sp", lambda e: e.dma_start(out=S32[:], in_=s_in), sem_c, writes=b_S + [b_const])
        P.dma("sp", lambda e: e.dma_start(out=uhalo[:], in_=uh_in), sem_c, writes=b_uh + [b_const])
        P.dma("sp", lambda e: e.dma_start(out=ghalo[:], in_=gh_in), sem_c, writes=b_gh + [b_const])
        P.dma("pool", lambda e: e.dma_start(out=wlrt[:], in_=wlr), sem_c, writes=[b_const])
        P.dma("pool", lambda e: e.dma_start(out=wa2t[0:17, :], in_=wa2e), sem_c, writes=[b_const])

        P.op("dve", lambda e: e.memset(onesD[:], 1.0 / D), writes=[b_const])
        P.op("dve", lambda e: e.memset(ones256[:], 1.0 / 256), writes=[b_const])
        P.op("dve", lambda e: e.memset(lrT[:], 1.0), writes=[b_lrT])
        P.op("dve", lambda e: e.memset(epst[:], EPS), writes=[b_const])

        P.op("act", lambda e: e.activation(out=csb[:], in_=cst[:], func=AF.Silu), reads=[b_const], writes=[b_mod])
        MISC = 5
        for j in range(96):
            s = load_piece(wada[j], 2048)
            fns = []
            for kc in range(KC):
                fns.append(lambda e, s=s, kc=kc, j=j: e.matmul(
                    banks[MISC][:, j:j + 1], slot(s)[:, kc * 128:(kc + 1) * 128], csb[:, kc:kc + 1],
                    start=(kc == 0), stop=(kc == KC - 1)))
            P.op("pe", fns, reads=[b_slot[s], b_mod], writes=[bankb[MISC]])
        P.op("dve", lambda e: e.tensor_tensor(out=modv[:], in0=banks[MISC][:, 0:96], in1=badat[:], op=ALU.add),
             reads=[bankb[MISC], b_const], writes=[b_mod])
        P.op("dve", lambda e: e.tensor_scalar(out=coef[:, 0:16], in0=modv[:, 16:32], scalar1=1.0, scalar2=None, op0=ALU.add),
             reads=[b_mod], writes=[b_coef])
        P.op("dve", lambda e: e.tensor_tensor(out=coef[:, 0:16], in0=coef[:, 0:16], in1=ngt[:, 0:16], op=ALU.mult),
             reads=[b_coef, b_const], writes=[b_coef])
        P.op("dve", lambda e: e.tensor_copy(out=coef[:, 16:32], in_=modv[:, 0:16]), reads=[b_mod, b_coef], writes=[b_coef])
        P.op("dve", lambda e: e.tensor_tensor(out=coef[:, 32:48], in0=modv[:, 32:48], in1=ngt[:, 16:32], op=ALU.mult),
             reads=[b_mod, b_coef], writes=[b_coef])
        P.op("dve", lambda e: e.tensor_scalar(out=coef[:, 48:64], in0=modv[:, 64:80], scalar1=1.0, scalar2=None, op0=ALU.add),
             reads=[b_mod, b_coef], writes=[b_coef])
        P.op("dve", lambda e: e.tensor_tensor(out=coef[:, 48:64], in0=coef[:, 48:64], in1=ngt[:, 32:48], op=ALU.mult),
             reads=[b_coef], writes=[b_coef])
        P.op("dve", lambda e: e.tensor_copy(out=coef[:, 64:80], in_=modv[:, 48:64]), reads=[b_mod, b_coef], writes=[b_coef])
        P.op("dve", lambda e: e.tensor_tensor(out=coef[:, 80:96], in0=modv[:, 80:96], in1=ngt[:, 48:64], op=ALU.mult),
             reads=[b_mod, b_coef], writes=[b_coef])

        mmring = [0]

        def next_bank():
            b = mmring[0] % 4
            mmring[0] += 1
            return b

        STAT = 4

        def norm_in(cofs):
            for c in range(KC):
                k = c % 2
                P.op("act", lambda e, c=c, k=k: e.activation(out=sq[:, k * T:(k + 1) * T], in_=xc(c), func=AF.Square),
                     reads=[b_x[c]], writes=[b_sq[k]])
                P.op("pe", lambda e, c=c, k=k: e.matmul(banks[STAT][:], onesD[:], sq[:, k * T:(k + 1) * T],
                                                          start=(c == 0), stop=(c == KC - 1)),
                     reads=[b_sq[k], b_const], writes=[bankb[STAT]])
            P.op("act", lambda e: e.activation(out=rstd[:, 0:T], in_=banks[STAT][:], func=AF.Sqrt, bias=epst[:, 0:1], scale=1.0),
                 reads=[bankb[STAT], b_const], writes=[b_rstd[0]])
            P.op("dve", lambda e: e.reciprocal(out=rstd[:, 0:T], in_=rstd[:, 0:T]), reads=[b_rstd[0]], writes=[b_rstd[0]])
            for c in range(KC):
                k = c % 2
                P.op("dve", lambda e, c=c, k=k: e.scalar_tensor_tensor(
                    out=tmpf[:, k * T:(k + 1) * T], in0=xc(c), scalar=coef[:, cofs + c:cofs + c + 1], in1=rstd[:, 0:T],
                    op0=ALU.mult, op1=ALU.mult), reads=[b_x[c], b_rstd[0], b_coef], writes=[b_tmp[k]])
                P.op("act", lambda e, c=c, k=k: e.activation(
                    out=hc(c), in_=tmpf[:, k * T:(k + 1) * T], func=AF.Identity,
                    bias=coef[:, cofs + 16 + c:cofs + 17 + c], scale=1.0), reads=[b_tmp[k], b_coef], writes=[b_h[c]])

        def mm_std(src_ap, rhs_of, nk, rhs_bufs, ncols=None):
            s = load_piece(src_ap, nk * 128)
            b = next_bank()
            fns = []
            for kc in range(nk):
                fns.append(lambda e, s=s, kc=kc, b=b: e.matmul(
                    banks[b][:], slot(s)[:, kc * 128:(kc + 1) * 128], rhs_of(kc), start=(kc == 0), stop=(kc == nk - 1)))
            P.op("pe", fns, reads=[b_slot[s]] + rhs_bufs, writes=[bankb[b]])
            return b

        def residual(gofs):
            P.op("act", lambda e: e.activation(out=rstd[:, T:2 * T], in_=banks[STAT][:], func=AF.Sqrt, bias=epst[:, 0:1], scale=1.0),
                 reads=[bankb[STAT], b_const], writes=[b_rstd[1]])
            P.op("dve", lambda e: e.reciprocal(out=rstd[:, T:2 * T], in_=rstd[:, T:2 * T]), reads=[b_rstd[1]], writes=[b_rstd[1]])
            for c in range(KC):
                k = 2 + c % 2
                P.op("dve", lambda e, c=c, k=k: e.scalar_tensor_tensor(
                    out=tmpf[:, k * T:(k + 1) * T], in0=ybuf[:, c * T:(c + 1) * T], scalar=coef[:, gofs + c:gofs + c + 1],
                    in1=rstd[:, T:2 * T], op0=ALU.mult, op1=ALU.mult), reads=[b_y[c], b_rstd[1], b_coef], writes=[b_tmp[k]])
                P.op("dve", lambda e, c=c, k=k: e.tensor_tensor(out=xc(c), in0=xc(c), in1=tmpf[:, k * T:(k + 1) * T], op=ALU.add),
                     reads=[b_tmp[k], b_x[c]], writes=[b_x[c]])

        def out_proj_to_y(j, b):
            k = j % 2
            P.op("act", lambda e, j=j, b=b: e.activation(out=ybuf[:, j * T:(j + 1) * T], in_=banks[b][:], func=AF.Identity),
                 reads=[bankb[b]], writes=[b_y[j]])
            P.op("act", lambda e, b=b, k=k: e.activation(out=sq[:, k * T:(k + 1) * T], in_=banks[b][:], func=AF.Square),
                 reads=[bankb[b]], writes=[b_sq[k]])
            P.op("pe", lambda e, j=j, k=k: e.matmul(banks[STAT][:], onesD[:], sq[:, k * T:(k + 1) * T],
                                                      start=(j == 0), stop=(j == KC - 1)),
                 reads=[b_sq[k], b_const], writes=[bankb[STAT]])

        def dbg(name, ap, bufs, eng="sp"):
            if debug:
                P.dma(eng, lambda e: e.dma_start(out=dbg_out[name], in_=ap), sem_o, reads=bufs)

        for step in range(n_steps):
            for qd in range(4):
                P.dma("sp", lambda e, qd=qd, step=step: e.dma_start(
                    out=xres[:, qd * 2048:(qd + 1) * 2048], in_=xin[step][:, qd * 2048:(qd + 1) * 2048]),
                    sem_x, writes=b_x[qd * 4:(qd + 1) * 4])

            norm_in(0)
            dbg("d_h", hbuf[:].bitcast(F32), b_h)
            if step == 0:
                dbg("d_mod", modv[:], [b_mod])

            fns = []
            for kc in range(KC):
                fns.append(lambda e, kc=kc: e.matmul(banks[MISC][0:16, :], wlrt[:, kc * 16:(kc + 1) * 16], hc(kc),
                                                      start=(kc == 0), stop=(kc == KC - 1)))
            P.op("pe", fns, reads=b_h + [b_const], writes=[bankb[MISC]])
            P.op("act", lambda e: e.activation(out=lrT[0:16, :], in_=banks[MISC][0:16, :], func=AF.Identity),
                 reads=[bankb[MISC]], writes=[b_lrT])
            for tb in range(4):
                b = next_bank()
                P.op("pe", lambda e, tb=tb, b=b: e.matmul(banks[b][:], lrT[0:17, tb * 128:(tb + 1) * 128], wa2t[0:17, :],
                                                           start=True, stop=True),
                     reads=[b_lrT, b_const], writes=[bankb[b]])
                k = tb % 2
                P.op("act", lambda e, b=b, k=k: e.activation(out=tmpf[:, k * T:(k + 1) * T], in_=banks[b][:], func=AF.Exp, scale=-1.0),
                     reads=[bankb[b]], writes=[b_tmp[k]])
                P.op("act", lambda e, tb=tb, k=k: e.activation(out=laf[:, tb * T:(tb + 1) * T], in_=tmpf[:, k * T:(k + 1) * T],
                                                                 func=AF.Ln, bias=1.0, scale=1.0),
                     reads=[b_tmp[k]], writes=[b_la[tb]])
            dbg("d_la", laf, b_la)
            for tb in range(4):
                b = next_bank()
                P.op("pe", lambda e, tb=tb, b=b: e.matmul(banks[b][:], maskU[:], laf[:, tb * T:(tb + 1) * T], start=True, stop=True),
                     reads=[b_la[tb], b_const], writes=[bankb[b]])
                P.op("act", lambda e, tb=tb, b=b: e.activation(out=Ef[:, tb * T:(tb + 1) * T], in_=banks[b][:], func=AF.Exp, scale=-1.0 / 16),
                     reads=[bankb[b]], writes=[b_E])
            dbg("d_E", Ef, [b_E])
            fns = []
            for tb in range(4):
                for hd in range(4):
                    col = (tb * 4 + hd) * 2
                    fns.append(lambda e, tb=tb, hd=hd, col=col: e.matmul(
                        banks[MISC][:, col:col + 2], laf[:, tb * T + hd * 128: tb * T + (hd + 1) * 128], cindt[:],
                        start=True, stop=True))
            P.op("pe", fns, reads=b_la + [b_const], writes=[bankb[MISC]])
            P.op("act", lambda e: e.activation(out=decay[:], in_=banks[MISC][:, 0:32], func=AF.Exp, scale=-1.0 / 16),
                 reads=[bankb[MISC]], writes=[b_decay])
            dbg("d_dec", decay[:], [b_decay])

            def mm_tok(piece):
                s = load_piece(win[piece], 2048)
                b = next_bank()
                fns = []
                for tb in range(4):
                    for kc in range(KC):
                        fns.append(lambda e, s=s, b=b, tb=tb, kc=kc: e.matmul(
                            banks[b][:, tb * 128:(tb + 1) * 128], hbuf[:, kc * T + tb * 128: kc * T + (tb + 1) * 128],
                            slot(s)[:, kc * 128:(kc + 1) * 128], start=(kc == 0), stop=(kc == KC - 1)))
                P.op("pe", fns, reads=[b_slot[s]] + b_h, writes=[bankb[b]])
                return b

            kd3 = kdec.rearrange("p (tb c) -> p tb c", tb=4)
            E3 = Ef.rearrange("p (tb c) -> p tb c", tb=4)
            v3 = vtm.rearrange("p (tb c) -> p tb c", tb=4)
            for hd in range(4):
                b = mm_tok(PK + hd)
                P.op("dve", lambda e, b=b, hd=hd: e.tensor_tensor(
                    out=kd3[:, :, hd * 128:(hd + 1) * 128], in0=banks[b][:].rearrange("p (tb c) -> p tb c", tb=4),
                    in1=E3[:, :, hd * 128:(hd + 1) * 128], op=ALU.mult), reads=[bankb[b], b_E], writes=[b_kdec])
            for jv in range(8):
                b = mm_tok(PV + jv)
                P.op("act", lambda e, b=b, jv=jv: e.activation(
                    out=v3[:, :, jv * 128:(jv + 1) * 128], in_=banks[b][:].rearrange("p (tb c) -> p tb c", tb=4),
                    func=AF.Identity), reads=[bankb[b]], writes=[b_v])
            for hd in range(4):
                b = mm_std(win[PQ + hd], hc, KC, b_h)
                P.op("act", lambda e, b=b, hd=hd: e.activation(out=qT[:, hd * T:(hd + 1) * T], in_=banks[b][:], func=AF.Identity,
                                                                 scale=float(128 ** -0.5)), reads=[bankb[b]], writes=[b_q[hd]])
            dbg("d_q", qT.bitcast(F32), b_q)
            dbg("d_k", kdec.bitcast(F32), [b_kdec])
            dbg("d_v", vtm.bitcast(F32), [b_v])

            o3 = oT.rearrange("p (c t) -> p c t", c=8)
            for ci in range(8):
                tb, half = ci // 2, ci % 2
                r0 = half * 64
                for hd in range(4):
                    P.op("pe", lambda e, hd=hd, tb=tb, r0=r0: e.matmul(
                        banks[hd][:, 0:256], kdec[r0:r0 + 64, tb * 512 + hd * 128: tb * 512 + (hd + 1) * 128],
                        vtm[r0:r0 + 64, tb * 1024 + hd * 256: tb * 1024 + (hd + 1) * 256], start=True, stop=True),
                        reads=[b_kdec, b_v], writes=[bankb[hd]])
                for hd in range(4):
                    dcol = (tb * 4 + hd) * 2 + half
                    P.op("dve", lambda e, hd=hd, dcol=dcol: e.scalar_tensor_tensor(
                        out=S32[:, hd * 256:(hd + 1) * 256], in0=S32[:, hd * 256:(hd + 1) * 256], scalar=decay[:, dcol:dcol + 1],
                        in1=banks[hd][:, 0:256], op0=ALU.mult, op1=ALU.add), reads=[bankb[hd], b_decay, b_S[hd]], writes=[b_S[hd]])
                    kk = ci % 2
                    P.op("act", lambda e, hd=hd, kk=kk: e.activation(
                        out=Sb[:, (hd * 2 + kk) * 256:(hd * 2 + kk + 1) * 256], in_=S32[:, hd * 256:(hd + 1) * 256], func=AF.Identity),
                        reads=[b_S[hd]], writes=[b_Sb[hd][kk]])
                for hd in range(4):
                    kk = ci % 2
                    ob = 4 + hd
                    fns = []
                    for hf in range(2):
                        fns.append(lambda e, hd=hd, kk=kk, hf=hf, ob=ob, ci=ci: e.matmul(
                            banks[ob][:, hf * 64:(hf + 1) * 64],
                            Sb[:, (hd * 2 + kk) * 256 + hf * 128:(hd * 2 + kk) * 256 + (hf + 1) * 128],
                            qT[:, hd * T + ci * 64: hd * T + (ci + 1) * 64], start=True, stop=True))
                    P.op("pe", fns, reads=[b_Sb[hd][kk], b_q[hd]], writes=[bankb[ob]])
                    P.op("act", lambda e, hd=hd, ob=ob, ci=ci: e.activation(
                        out=o3[:, hd * 2:hd * 2 + 2, ci * 64:(ci + 1) * 64],
                        in_=banks[ob][:, 0:128].rearrange("p (h t) -> p h t", h=2), func=AF.Identity),
                        reads=[bankb[ob]], writes=[b_o[hd * 2], b_o[hd * 2 + 1]])
            dbg("d_o", oT, b_o)

            for jr in range(8):
                b = mm_std(win[PR + jr], hc, KC, b_h)
                P.op("act", lambda e, b=b, jr=jr: e.activation(out=rs[:, jr * T:(jr + 1) * T], in_=banks[b][:], func=AF.Silu),
                     reads=[bankb[b]], writes=[b_rs[jr]])
            for hd in range(4):
                for hf in range(2):
                    c = hd * 2 + hf
                    k = hf
                    P.op("act", lambda e, c=c, k=k: e.activation(out=sq[:, k * T:(k + 1) * T], in_=oT[:, c * T:(c + 1) * T], func=AF.Square),
                         reads=[b_o[c]], writes=[b_sq[k]])
                    P.op("pe", lambda e, k=k, hf=hf: e.matmul(banks[STAT][:], ones256[:], sq[:, k * T:(k + 1) * T],
                                                                start=(hf == 0), stop=(hf == 1)),
                         reads=[b_sq[k], b_const], writes=[bankb[STAT]])
                P.op("act", lambda e: e.activation(out=rstd[:, T:2 * T], in_=banks[STAT][:], func=AF.Sqrt, bias=epst[:, 0:1], scale=1.0),
                     reads=[bankb[STAT], b_const], writes=[b_rstd[1]])
                P.op("dve", lambda e: e.reciprocal(out=rstd[:, T:2 * T], in_=rstd[:, T:2 * T]), reads=[b_rstd[1]], writes=[b_rstd[1]])
                for hf in range(2):
                    c = hd * 2 + hf
                    k = 2 + hf
                    P.op("dve", lambda e, c=c, k=k, hf=hf: e.scalar_tensor_tensor(
                        out=tmpf[:, k * T:(k + 1) * T], in0=oT[:, c * T:(c + 1) * T], scalar=glagt[:, hf:hf + 1],
                        in1=rstd[:, T:2 * T], op0=ALU.mult, op1=ALU.mult), reads=[b_o[c], b_rstd[1], b_const], writes=[b_tmp[k]])
                    P.op("dve", lambda e, c=c, k=k: e.tensor_tensor(
                        out=gin[:, c * T:(c + 1) * T], in0=tmpf[:, k * T:(k + 1) * T], in1=rs[:, c * T:(c + 1) * T], op=ALU.mult),
                        reads=[b_tmp[k], b_rs[c]], writes=[b_gin[c]])
            dbg("d_gin", gin.bitcast(F32), b_gin)

            for c in range(8):
                k = c % 2
                ub = ubuf[:, k * (T + 2):(k + 1) * (T + 2)]
                cb_ = cbuf[:, k * T:(k + 1) * T]
                b1 = mm_std(win[PCC + c], hc, KC, b_h)
                P.op("act", lambda e, b1=b1, k=k: e.activation(out=tmpf[:, k * T:(k + 1) * T], in_=banks[b1][:], func=AF.Identity),
                     reads=[bankb[b1]], writes=[b_tmp[k]])
                b2 = mm_std(win[PCX + c], hc, KC, b_h)
                P.op("dve", lambda e, ub=ub, c=c: e.tensor_copy(out=ub[:, 0:2], in_=uhalo[:, 2 * c:2 * c + 2]),
                     reads=[b_uh[c]], writes=[b_ubuf[k]])
                P.op("dve", lambda e, ub=ub, b2=b2, k=k: e.tensor_tensor(out=ub[:, 2:T + 2], in0=banks[b2][:], in1=tmpf[:, k * T:(k + 1) * T], op=ALU.mult),
                     reads=[bankb[b2], b_tmp[k], b_ubuf[k]], writes=[b_ubuf[k]])
                P.op("dve", lambda e, ub=ub, c=c: e.tensor_copy(out=uhalo[:, 2 * c:2 * c + 2], in_=ub[:, T:T + 2]),
                     reads=[b_ubuf[k]], writes=[b_uh[c]])
                P.op("dve", lambda e, ub=ub, cb_=cb_, c=c: e.tensor_scalar(out=cb_, in0=ub[:, 0:T], scalar1=cmwt[:, 3 * c:3 * c + 1], scalar2=None, op0=ALU.mult),
                     reads=[b_ubuf[k], b_const], writes=[b_cbuf[k]])
                P.op("dve", lambda e, ub=ub, cb_=cb_, c=c: e.scalar_tensor_tensor(out=cb_, in0=ub[:, 1:T + 1], scalar=cmwt[:, 3 * c + 1:3 * c + 2], in1=cb_, op0=ALU.mult, op1=ALU.add),
                     reads=[b_ubuf[k], b_cbuf[k]], writes=[b_cbuf[k]])
                P.op("dve", lambda e, ub=ub, cb_=cb_, c=c: e.scalar_tensor_tensor(out=cb_, in0=ub[:, 2:T + 2], scalar=cmwt[:, 3 * c + 2:3 * c + 3], in1=cb_, op0=ALU.mult, op1=ALU.add),
                     reads=[b_ubuf[k], b_cbuf[k]], writes=[b_cbuf[k]])
                b3 = mm_std(win[PCB + c], hc, KC, b_h)
                P.op("dve", lambda e, b3=b3, cb_=cb_, c=c: e.tensor_tensor(out=cvb[:, c * T:(c + 1) * T], in0=banks[b3][:], in1=cb_, op=ALU.mult),
                     reads=[bankb[b3], b_cbuf[k]], writes=[b_cvb[c]])
            dbg("d_cvb", cvb.bitcast(F32), b_cvb)

            for j in range(16):
                ka, kb = (j % 2) * 2, (j % 2) * 2 + 1
                ta = tmpf[:, ka * T:(ka + 1) * T]
                tbv = tmpf[:, kb * T:(kb + 1) * T]
                b1 = mm_std(win[PGA + j], hc, KC, b_h)
                P.op("act", lambda e, b1=b1, ta=ta: e.activation(out=ta, in_=banks[b1][:], func=AF.Sigmoid),
                     reads=[bankb[b1]], writes=[b_tmp[ka]])
                b2 = mm_std(woutg[j], lambda kc: gin[:, kc * T:(kc + 1) * T], 8, b_gin)
                P.op("dve", lambda e, b2=b2, ta=ta: e.tensor_tensor(out=ta, in0=banks[b2][:], in1=ta, op=ALU.mult),
                     reads=[bankb[b2], b_tmp[ka]], writes=[b_tmp[ka]])
                b3 = mm_std(win[PGB + j], hc, KC, b_h)
                P.op("act", lambda e, b3=b3, tbv=tbv: e.activation(out=tbv, in_=banks[b3][:], func=AF.Sigmoid),
                     reads=[bankb[b3]], writes=[b_tmp[kb]])
                b4 = mm_std(woutc[j], lambda kc: cvb[:, kc * T:(kc + 1) * T], 8, b_cvb)
                P.op("dve", lambda e, b4=b4, tbv=tbv: e.tensor_tensor(out=tbv, in0=banks[b4][:], in1=tbv, op=ALU.mult),
                     reads=[bankb[b4], b_tmp[kb]], writes=[b_tmp[kb]])
                P.op("dve", lambda e, j=j, ta=ta, tbv=tbv: e.tensor_tensor(out=mbuf[:, j * T:(j + 1) * T], in0=ta, in1=tbv, op=ALU.add),
                     reads=[b_tmp[ka], b_tmp[kb]], writes=[b_m[j]])
            dbg("d_m", mbuf.bitcast(F32), b_m)

            for j in range(16):
                b = mm_std(wo[j], lambda kc: mbuf[:, kc * T:(kc + 1) * T], KC, b_m)
                out_proj_to_y(j, b)
            residual(32)
            dbg("d_xmid", xres[:], b_x)

            norm_in(48)
            for j in range(FC):
                k = j % 2
                ub = ubuf[:, k * (T + 2):(k + 1) * (T + 2)]
                cb_ = cbuf[:, k * T:(k + 1) * T]
                b1 = mm_std(wup[j], hc, KC, b_h)
                P.op("dve", lambda e, ub=ub, j=j: e.tensor_copy(out=ub[:, 0:2], in_=ghalo[:, 2 * j:2 * j + 2]),
                     reads=[b_gh[j]], writes=[b_ubuf[k]])
                P.op("act", lambda e, ub=ub, b1=b1: e.activation(out=ub[:, 2:T + 2], in_=banks[b1][:], func=AF.Identity),
                     reads=[bankb[b1], b_ubuf[k]], writes=[b_ubuf[k]])
                P.op("dve", lambda e, ub=ub, j=j: e.tensor_copy(out=ghalo[:, 2 * j:2 * j + 2], in_=ub[:, T:T + 2]),
                     reads=[b_ubuf[k]], writes=[b_gh[j]])
                P.op("dve", lambda e, ub=ub, cb_=cb_, j=j: e.tensor_scalar(out=cb_, in0=ub[:, 0:T], scalar1=fcwt[:, 3 * j:3 * j + 1], scalar2=None, op0=ALU.mult),
                     reads=[b_ubuf[k], b_const], writes=[b_cbuf[k]])
                P.op("dve", lambda e, ub=ub, cb_=cb_, j=j: e.scalar_tensor_tensor(out=cb_, in0=ub[:, 1:T + 1], scalar=fcwt[:, 3 * j + 1:3 * j + 2], in1=cb_, op0=ALU.mult, op1=ALU.add),
                     reads=[b_ubuf[k], b_cbuf[k]], writes=[b_cbuf[k]])
                P.op("dve", lambda e, ub=ub, cb_=cb_, j=j: e.scalar_tensor_tensor(out=cb_, in0=ub[:, 2:T + 2], scalar=fcwt[:, 3 * j + 2:3 * j + 3], in1=cb_, op0=ALU.mult, op1=ALU.add),
                     reads=[b_ubuf[k], b_cbuf[k]], writes=[b_cbuf[k]])
                P.op("act", lambda e, cb_=cb_, k=k: e.activation(out=tmpf[:, k * T:(k + 1) * T], in_=cb_, func=AF.Gelu),
                     reads=[b_cbuf[k]], writes=[b_tmp[k]])
                b2 = mm_std(wup[FC + j], hc, KC, b_h)
                P.op("dve", lambda e, b2=b2, j=j, k=k: e.tensor_tensor(out=actb[:, j * T:(j + 1) * T], in0=banks[b2][:], in1=tmpf[:, k * T:(k + 1) * T], op=ALU.mult),
                     reads=[bankb[b2], b_tmp[k]], writes=[b_act[j]])
            for j in range(16):
                b = next_bank()
                for g in range(4):
                    s = load_piece(wdn[j * 4 + g], 1408)
                    fns = []
                    for kk in range(11):
                        kc = g * 11 + kk
                        fns.append(lambda e, s=s, kk=kk, kc=kc, b=b: e.matmul(
                            banks[b][:], slot(s)[:, kk * 128:(kk + 1) * 128], actb[:, kc * T:(kc + 1) * T],
                            start=(kc == 0), stop=(kc == FC - 1)))
                    P.op("pe", fns, reads=[b_slot[s]] + b_act[g * 11:(g + 1) * 11], writes=[bankb[b]])
                out_proj_to_y(j, b)
            residual(80)

            for qd in range(4):
                P.dma("sp", lambda e, qd=qd, step=step: e.dma_start(
                    out=xout[step][:, qd * 2048:(qd + 1) * 2048], in_=xres[:, qd * 2048:(qd + 1) * 2048]),
                    sem_o, reads=b_x[qd * 4:(qd + 1) * 4])

        P.dma("sp", lambda e: e.dma_start(out=s_out, in_=S32[:]), sem_o, reads=b_S)
        P.dma("sp", lambda e: e.dma_start(out=uh_out, in_=uhalo[:]), sem_o, reads=b_uh)
        last = P.dma("sp", lambda e: e.dma_start(out=gh_out, in_=ghalo[:]), sem_o, reads=b_gh)

        P.finalize(engsem)
        final_o = P.dma_counts[id(sem_o)]

        block = E(nc.Block())

        @block.tensor
        def _(e):
            P.replay("pe", e)

        @block.scalar
        def _(e):
            P.replay("act", e)

        @block.vector
        def _(e):
            P.replay("dve", e)

        @block.gpsimd
        def _(e):
            P.replay("pool", e)

        @block.sync
        def _(e):
            P.replay("sp", e)
            e.wait_ge(sem_o, final_o)

    return nc


def _pieces(w, col0s, nk):
    out = np.empty((len(col0s), 128, nk * 128), np.float32)
    wk = w.reshape(nk, 128, w.shape[1])
    for j, c0 in enumerate(col0s):
        out[j] = wk[:, :, c0:c0 + 128].transpose(1, 0, 2).reshape(128, nk * 128)
    return out


def _fm(v):
    return np.ascontiguousarray(v.reshape(-1, 128).T)


_MASKU = None


def _consts():
    global _MASKU
    if _MASKU is None:
        j = np.arange(128)[:, None]
        l = np.arange(128)[None, :]
        mu = ((j > l) & ((j // 64) == (l // 64))).astype(np.float32)
        ci = np.zeros((128, 2), np.float32)
        ci[:64, 0] = 1.0
        ci[64:, 1] = 1.0
        _MASKU = (mu, ci)
    return _MASKU


def layer_maps(l, w_ada, b_ada, norm_g, w_in, w_a2, b_a2, gla_norm_g, w_out_gla, conv_mix_w,
               w_out_conv, w_o, w_up, ffn_conv_w, w_down):
    col0 = [j * 128 for j in range(24)] + [3088 + j * 128 for j in range(56)]
    mu, ci = _consts()
    m = {}
    m["wada"] = _pieces(w_ada[l], [j * 128 for j in range(96)], 16)
    m["bada"] = _fm(b_ada[l])
    m["ng"] = np.ascontiguousarray(np.concatenate([_fm(norm_g[l, i]) for i in range(4)], axis=1))
    m["win"] = _pieces(w_in[l], col0, 16)
    m["wlr"] = np.ascontiguousarray(w_in[l][:, 3072:3088].reshape(16, 128, 16).transpose(1, 0, 2).reshape(128, 256))
    m["wa2e"] = np.ascontiguousarray(np.concatenate([w_a2[l], b_a2[l][None, :]], axis=0))
    m["glag"] = _fm(gla_norm_g[l])
    m["woutg"] = _pieces(w_out_gla[l], [j * 128 for j in range(16)], 8)
    m["woutc"] = _pieces(w_out_conv[l], [j * 128 for j in range(16)], 8)
    m["cmw"] = np.ascontiguousarray(conv_mix_w[l].reshape(3, 8, 128).transpose(2, 1, 0).reshape(128, 24))
    m["wo"] = _pieces(w_o[l], [j * 128 for j in range(16)], 16)
    m["wup"] = _pieces(w_up[l], [j * 128 for j in range(88)], 16)
    m["fcw"] = np.ascontiguousarray(ffn_conv_w[l].reshape(3, FC, 128).transpose(2, 1, 0).reshape(128, 132))
    wd = _pieces(w_down[l], [j * 128 for j in range(16)], FC)
    m["wdn"] = np.ascontiguousarray(wd.reshape(16, 128, 4, 1408).transpose(0, 2, 1, 3).reshape(64, 128, 1408))
    m["masku"] = mu
    m["cind"] = ci
    return m


def x_tiles(xb):
    s = xb.shape[0]
    return np.ascontiguousarray(xb.reshape(s // T, T, KC, 128).transpose(0, 3, 2, 1).reshape(s // T, 128, KC * T))


def x_untile(t):
    n = t.shape[0]
    return np.ascontiguousarray(t.reshape(n, 128, KC, T).transpose(0, 3, 2, 1).reshape(n * T, D))


_NC_CACHE = {}


def _get_nc(n_steps):
    if n_steps not in _NC_CACHE:
        _NC_CACHE[n_steps] = build(n_steps)
    return _NC_CACHE[n_steps]


def kernel(x, c, w_ada, b_ada, norm_g, w_in, w_a2, b_a2, gla_norm_g, w_out_gla,
           conv_mix_w, w_out_conv, w_o, w_up, ffn_conv_w, w_down):
    args = [np.asarray(a, np.float32) for a in (w_ada, b_ada, norm_g, w_in, w_a2, b_a2, gla_norm_g, w_out_gla,
                                                conv_mix_w, w_out_conv, w_o, w_up, ffn_conv_w, w_down)]
    x = np.asarray(x, np.float32)
    c = np.asarray(c, np.float32)
    B = x.shape[0]
    nc = _get_nc(NTILE)
    cur = [x_tiles(x[b]) for b in range(B)]
    for l in range(2):
        lm = layer_maps(l, *args)
        in_maps = []
        for b in range(B):
            m = dict(lm)
            m["cvec"] = _fm(c[b])
            m["xin"] = cur[b]
            m["s_in"] = np.zeros((128, 1024), np.float32)
            m["uh_in"] = np.zeros((128, 16), np.float32)
            m["gh_in"] = np.zeros((128, 88), np.float32)
            in_maps.append(m)
        res = run_bass_kernel_spmd(nc, in_maps, core_ids=list(range(B)))
        cur = [np.asarray(res.results[b]["xout"]) for b in range(B)]
    out = np.stack([x_untile(cur[b]) for b in range(B)], axis=0)
    return out.astype(np.float32)
```

```python
import numpy as np
from contextlib import ExitStack
import concourse.bass as bass
import concourse.mybir as mybir
from concourse.bass_utils import run_bass_kernel_spmd

F32 = mybir.dt.float32
BF16 = mybir.dt.bfloat16
AF = mybir.ActivationFunctionType
ALU = mybir.AluOpType

D = 2048
KC = 16
T = 512
NTILE = 8
DFF = 5632
FC = 44
EPS = 1e-6
NSLOT = 6
SLOTW = 2048

PQ, PK, PV, PR, PCB, PCC, PCX, PGA, PGB = 0, 4, 8, 16, 24, 32, 40, 48, 64


class Op:
    __slots__ = ("eng", "fns", "deps", "signal", "val", "dma_sem", "dma_val", "inc")

    def __init__(self, eng, fns, deps):
        self.eng = eng
        self.fns = fns
        self.deps = deps
        self.signal = False
        self.val = None
        self.dma_sem = None
        self.dma_val = None
        self.inc = 16


class Buf:
    __slots__ = ("w", "r", "lo", "hi", "psum", "last")

    def __init__(self, lo=None, hi=None, psum=False):
        self.w = None
        self.r = []
        self.lo = lo
        self.hi = hi
        self.psum = psum
        self.last = None


class Prog:
    def __init__(self):
        self.ops = {"pe": [], "act": [], "dve": [], "pool": [], "sp": []}
        self.arena_bufs = []
        self.dma_counts = {}

    def abuf(self, lo, hi):
        b = Buf(lo, hi)
        self.arena_bufs.append(b)
        return b

    def _deps(self, eng, reads, writes):
        deps = []
        for b in reads:
            if b.psum:
                if b.last is not None and b.last.eng != eng:
                    deps.append(b.last)
            elif b.w is not None:
                deps.append(b.w)
        for b in writes:
            if b.psum:
                if b.last is not None and b.last.eng != eng:
                    deps.append(b.last)
                continue
            if b.w is not None:
                deps.append(b.w)
            deps.extend(b.r)
            if b.lo is not None:
                for o in self.arena_bufs:
                    if o is not b and o.lo < b.hi and b.lo < o.hi:
                        if o.w is not None:
                            deps.append(o.w)
                        deps.extend(o.r)
        return deps

    def _note(self, op, reads, writes):
        for b in reads:
            if b.psum:
                b.last = op
            else:
                b.r.append(op)
        for b in writes:
            if b.psum:
                b.last = op
                continue
            b.w = op
            b.r = []
            if b.lo is not None:
                for o in self.arena_bufs:
                    if o is not b and o.lo < b.hi and b.lo < o.hi:
                        o.w = None
                        o.r = []

    def op(self, eng, fns, reads=(), writes=(), deps=()):
        if not isinstance(fns, (list, tuple)):
            fns = [fns]
        d = self._deps(eng, reads, writes) + [x for x in deps if x is not None]
        o = Op(eng, list(fns), d)
        self.ops[eng].append(o)
        self._note(o, reads, writes)
        return o

    def dma(self, eng, fn, sem, reads=(), writes=(), deps=(), inc=16):
        o = self.op(eng, fn, reads, writes, deps)
        c = self.dma_counts.get(id(sem), 0) + inc
        self.dma_counts[id(sem)] = c
        o.dma_sem = sem
        o.dma_val = c
        o.inc = inc
        return o

    def finalize(self, engsem):
        for lst in self.ops.values():
            for o in lst:
                for dd in o.deps:
                    if dd.dma_sem is None:
                        dd.signal = True
        for eng, lst in self.ops.items():
            c = 0
            for o in lst:
                if o.signal:
                    c += 1
                    o.val = c
        self.engsem = engsem

    def replay(self, eng, e):
        seen = {}
        for o in self.ops[eng]:
            for dd in o.deps:
                if dd.dma_sem is not None:
                    sem, val = dd.dma_sem, dd.dma_val
                else:
                    if dd.eng == eng and False:
                        continue
                    sem, val = self.engsem[dd.eng], dd.val
                k = id(sem)
                if seen.get(k, 0) >= val:
                    continue
                seen[k] = val
                e.wait_ge(sem, val)
            ins = None
            for f in o.fns:
                ins = f(e)
            if o.dma_sem is not None:
                if o.inc == 16:
                    ins.then_inc(o.dma_sem, 16)
                else:
                    ins.then_inc(o.dma_sem)
            elif o.signal:
                ins.then_inc(self.engsem[eng], 1)


def build(n_steps=1, debug=False, fused=False):
    nc = bass.Bass("TRN2", target_bir_lowering=False)

    def din(name, shape):
        return nc.dram_tensor(name, list(shape), F32, kind="ExternalInput").ap()

    def dout(name, shape):
        return nc.dram_tensor(name, list(shape), F32, kind="ExternalOutput").ap()

    xin = din("xin", [n_steps, 128, KC * T])
    cvec = din("cvec", [128, KC])
    wada = din("wada", [96, 128, 2048])
    bada = din("bada", [128, 96])
    ngd = din("ng", [128, 64])
    win = din("win", [80, 128, 2048])
    wlr = din("wlr", [128, 256])
    wa2e = din("wa2e", [17, 512])
    glag = din("glag", [128, 2])
    woutg = din("woutg", [16, 128, 1024])
    woutc = din("woutc", [16, 128, 1024])
    cmw = din("cmw", [128, 24])
    wo = din("wo", [16, 128, 2048])
    wup = din("wup", [88, 128, 2048])
    fcw = din("fcw", [128, 132])
    wdn = din("wdn", [64, 128, 1408])
    masku = din("masku", [128, 128])
    cind = din("cind", [128, 2])
    s_in = din("s_in", [128, 1024])
    uh_in = din("uh_in", [128, 16])
    gh_in = din("gh_in", [128, 88])
    flags = din("flags", [128, 2])
    if fused:
        outbox = [nc.dram_tensor(f"outbox{i}", [128, 4096], F32) for i in range(2)]
        gath = [nc.dram_tensor(f"gath{i}", [256, 4096], F32) for i in range(2)]
    xout = dout("xout", [n_steps, 128, KC * T])
    s_out = dout("s_out", [128, 1024])
    uh_out = dout("uh_out", [128, 16])
    gh_out = dout("gh_out", [128, 88])
    dbg_out = {}
    if debug:
        for nm, w in [("d_h", KC * T // 2), ("d_la", 2048), ("d_E", 2048), ("d_dec", 32), ("d_q", 1024),
                      ("d_k", 1024), ("d_v", 2048), ("d_o", 4096), ("d_gin", 2048), ("d_cvb", 2048),
                      ("d_m", 4096), ("d_xmid", KC * T), ("d_mod", 96)]:
            dbg_out[nm] = dout(nm, [128, w])

    P = Prog()
    with ExitStack() as es:
        E = es.enter_context

        def sb(name, shape, dt):
            return E(nc.sbuf_tensor(name, list(shape), dt))

        xres = sb("xres", [128, KC * T], F32)
        hbuf = sb("hbuf", [128, KC * T], BF16)
        arena = sb("arena", [128, 40960], BF16)
        ring = sb("ring", [128, NSLOT * SLOTW], BF16)
        sq = sb("sq", [128, 2 * T], BF16)
        rstd = sb("rstd", [128, 2 * T], F32)
        tmpf = sb("tmpf", [128, 4 * T], F32)
        ubuf = sb("ubuf", [128, 2 * (T + 2)], F32)
        cbuf = sb("cbuf", [128, 2 * T], F32)
        S32 = sb("S32", [128, 1024], F32)
        Sb = sb("Sb", [128, 2048], BF16)
        lrT = sb("lrT", [32, T], BF16)
        onesD = sb("onesD", [128, 128], BF16)
        ones256 = sb("ones256", [128, 128], BF16)
        maskU = sb("maskU", [128, 128], F32)
        cindt = sb("cindt", [128, 2], F32)
        wa2t = sb("wa2t", [32, 512], BF16)
        wlrt = sb("wlrt", [128, 256], BF16)
        modv = sb("modv", [128, 96], F32)
        badat = sb("badat", [128, 96], F32)
        ngt = sb("ngt", [128, 64], F32)
        coef = sb("coef", [128, 96], F32)
        glagt = sb("glagt", [128, 2], F32)
        cmwt = sb("cmwt", [128, 24], F32)
        fcwt = sb("fcwt", [128, 132], F32)
        cst = sb("cst", [128, KC], F32)
        csb = sb("csb", [128, KC], BF16)
        decay = sb("decay", [128, 32], F32)
        uhalo = sb("uhalo", [128, 16], F32)
        ghalo = sb("ghalo", [128, 88], F32)
        epst = sb("epst", [128, 1], F32)
        flagt = sb("flagt", [128, 2], F32)

        banks = [E(nc.psum_tensor(f"bank{i}", [128, 512], F32)) for i in range(8)]
        bankb = [Buf(psum=True) for _ in range(8)]

        engsem = {k: E(nc.semaphore(f"sem_{k}")) for k in ["pe", "act", "dve", "pool", "sp"]}
        slot_sem = [E(nc.semaphore(f"slot{i}")) for i in range(NSLOT)]
        sem_x = E(nc.semaphore("sem_x"))
        sem_c = E(nc.semaphore("sem_c"))
        sem_o = E(nc.semaphore("sem_o"))
        sem_b = E(nc.semaphore("sem_b"))
        sem_cc = E(nc.semaphore("sem_cc"))
        sem_r = E(nc.semaphore("sem_r"))
        b_outbox = [Buf(), Buf()]
        b_gath = [Buf(), Buf()]

        A0, B0 = 0, 16384

        def aview(lo, n, dt=BF16):
            ap = arena[:, lo:lo + n]
            return ap.bitcast(F32) if dt == F32 else ap

        qT = aview(A0 + 0, 2048)
        kdec = aview(A0 + 2048, 2048)
        vtm = aview(A0 + 4096, 4096)
        Ef = aview(A0 + 8192, 4096, F32)
        laf = aview(A0 + 12288, 4096, F32)
        oT = aview(A0 + 8192, 8192, F32)
        ybuf = aview(A0, 16384, F32)
        rs = aview(B0 + 0, 4096)
        gin = aview(B0 + 4096, 4096)
        cvb = aview(B0 + 8192, 4096)
        mbuf = aview(B0 + 12288, 8192)
        actb = aview(B0, FC * T)

        def ab(lo, n):
            return P.abuf(lo, lo + n)

        b_q = [ab(A0 + h * 512, 512) for h in range(4)]
        b_kdec = ab(A0 + 2048, 2048)
        b_v = ab(A0 + 4096, 4096)
        b_E = ab(A0 + 8192, 4096)
        b_la = [ab(A0 + 12288 + tb * 1024, 1024) for tb in range(4)]
        b_o = [ab(A0 + 8192 + c * 1024, 1024) for c in range(8)]
        b_y = [ab(A0 + c * 1024, 1024) for c in range(16)]
        b_rs = [ab(B0 + c * 512, 512) for c in range(8)]
        b_gin = [ab(B0 + 4096 + c * 512, 512) for c in range(8)]
        b_cvb = [ab(B0 + 8192 + c * 512, 512) for c in range(8)]
        b_m = [ab(B0 + 12288 + c * 512, 512) for c in range(16)]
        b_act = [ab(B0 + c * 512, 512) for c in range(FC)]

        b_x = [Buf() for _ in range(KC)]
        b_h = [Buf() for _ in range(KC)]
        b_slot = [Buf() for _ in range(NSLOT)]
        b_sq = [Buf(), Buf()]
        b_rstd = [Buf(), Buf()]
        b_tmp = [Buf() for _ in range(4)]
        b_ubuf = [Buf(), Buf()]
        b_cbuf = [Buf(), Buf()]
        b_S = [Buf() for _ in range(4)]
        b_Sb = [[Buf(), Buf()] for _ in range(4)]
        b_lrT = Buf()
        b_const = Buf()
        b_mod = Buf()
        b_coef = Buf()
        b_decay = Buf()
        b_uh = [Buf() for _ in range(8)]
        b_gh = [Buf() for _ in range(FC)]

        def xc(c):
            return xres[:, c * T:(c + 1) * T]

        def hc(c):
            return hbuf[:, c * T:(c + 1) * T]

        def slot(i, n=SLOTW):
            return ring[:, i * SLOTW:i * SLOTW + n]

        piece_ctr = [0]

        def load_piece(src_ap, ncols):
            i = piece_ctr[0]
            piece_ctr[0] += 1
            s = i % NSLOT
            P.dma("pool", lambda e, s=s, src_ap=src_ap, ncols=ncols: e.dma_start(out=slot(s, ncols), in_=src_ap),
                  slot_sem[s], writes=[b_slot[s]])
            return s

        def cdma(dst, src, eng="sp", sem=None):
            return P.dma(eng, lambda e: e.dma_start(out=dst, in_=src), sem or sem_c, writes=[b_const])

        cdma(cst[:], cvec)
        cdma(badat[:], bada)
        cdma(ngt[:], ngd)
        cdma(glagt[:], glag)
        cdma(cmwt[:], cmw)
        cdma(fcwt[:], fcw)
        cdma(maskU[:], masku)
        cdma(cindt[:], cind)
        cdma(flagt[:], flags)
        P.dma("sp", lambda e: e.dma_start(out=S32[:], in_=s_in), sem_c, writes=b_S + [b_const])
        P.dma("sp", lambda e: e.dma_start(out=uhalo[:], in_=uh_in), sem_c, writes=b_uh + [b_const])
        P.dma("sp", lambda e: e.dma_start(out=ghalo[:], in_=gh_in), sem_c, writes=b_gh + [b_const])
        P.dma("pool", lambda e: e.dma_start(out=wlrt[:], in_=wlr), sem_c, writes=[b_const])
        P.dma("pool", lambda e: e.dma_start(out=wa2t[0:17, :], in_=wa2e), sem_c, writes=[b_const])

        P.op("dve", lambda e: e.memset(onesD[:], 1.0 / D), writes=[b_const])
        P.op("dve", lambda e: e.memset(ones256[:], 1.0 / 256), writes=[b_const])
        P.op("dve", lambda e: e.memset(lrT[:], 1.0), writes=[b_lrT])
        P.op("dve", lambda e: e.memset(epst[:], EPS), writes=[b_const])

        P.op("act", lambda e: e.activation(out=csb[:], in_=cst[:], func=AF.Silu), reads=[b_const], writes=[b_mod])
        MISC = 5
        for j in range(96):
            s = load_piece(wada[j], 2048)
            fns = []
            for kc in range(KC):
                fns.append(lambda e, s=s, kc=kc, j=j: e.matmul(
                    banks[MISC][:, j:j + 1], slot(s)[:, kc * 128:(kc + 1) * 128], csb[:, kc:kc + 1],
                    start=(kc == 0), stop=(kc == KC - 1)))
            P.op("pe", fns, reads=[b_slot[s], b_mod], writes=[bankb[MISC]])
        P.op("dve", lambda e: e.tensor_tensor(out=modv[:], in0=banks[MISC][:, 0:96], in1=badat[:], op=ALU.add),
             reads=[bankb[MISC], b_const], writes=[b_mod])
        P.op("dve", lambda e: e.tensor_scalar(out=coef[:, 0:16], in0=modv[:, 16:32], scalar1=1.0, scalar2=None, op0=ALU.add),
             reads=[b_mod], writes=[b_coef])
        P.op("dve", lambda e: e.tensor_tensor(out=coef[:, 0:16], in0=coef[:, 0:16], in1=ngt[:, 0:16], op=ALU.mult),
             reads=[b_coef, b_const], writes=[b_coef])
        P.op("dve", lambda e: e.tensor_copy(out=coef[:, 16:32], in_=modv[:, 0:16]), reads=[b_mod, b_coef], writes=[b_coef])
        P.op("dve", lambda e: e.tensor_tensor(out=coef[:, 32:48], in0=modv[:, 32:48], in1=ngt[:, 16:32], op=ALU.mult),
             reads=[b_mod, b_coef], writes=[b_coef])
        P.op("dve", lambda e: e.tensor_scalar(out=coef[:, 48:64], in0=modv[:, 64:80], scalar1=1.0, scalar2=None, op0=ALU.add),
             reads=[b_mod, b_coef], writes=[b_coef])
        P.op("dve", lambda e: e.tensor_tensor(out=coef[:, 48:64], in0=coef[:, 48:64], in1=ngt[:, 32:48], op=ALU.mult),
             reads=[b_coef], writes=[b_coef])
        P.op("dve", lambda e: e.tensor_copy(out=coef[:, 64:80], in_=modv[:, 48:64]), reads=[b_mod, b_coef], writes=[b_coef])
        P.op("dve", lambda e: e.tensor_tensor(out=coef[:, 80:96], in0=modv[:, 80:96], in1=ngt[:, 48:64], op=ALU.mult),
             reads=[b_mod, b_coef], writes=[b_coef])

        mmring = [0]

        def next_bank():
            b = mmring[0] % 4
            mmring[0] += 1
            return b

        STAT = 4

        def norm_in(cofs):
            for c in range(KC):
                k = c % 2
                P.op("act", lambda e, c=c, k=k: e.activation(out=sq[:, k * T:(k + 1) * T], in_=xc(c), func=AF.Square),
                     reads=[b_x[c]], writes=[b_sq[k]])
                P.op("pe", lambda e, c=c, k=k: e.matmul(banks[STAT][:], onesD[:], sq[:, k * T:(k + 1) * T],
                                                          start=(c == 0), stop=(c == KC - 1)),
                     reads=[b_sq[k], b_const], writes=[bankb[STAT]])
            P.op("act", lambda e: e.activation(out=rstd[:, 0:T], in_=banks[STAT][:], func=AF.Sqrt, bias=epst[:, 0:1], scale=1.0),
                 reads=[bankb[STAT], b_const], writes=[b_rstd[0]])
            P.op("dve", lambda e: e.reciprocal(out=rstd[:, 0:T], in_=rstd[:, 0:T]), reads=[b_rstd[0]], writes=[b_rstd[0]])
            for c in range(KC):
                k = c % 2
                P.op("dve", lambda e, c=c, k=k: e.scalar_tensor_tensor(
                    out=tmpf[:, k * T:(k + 1) * T], in0=xc(c), scalar=coef[:, cofs + c:cofs + c + 1], in1=rstd[:, 0:T],
                    op0=ALU.mult, op1=ALU.mult), reads=[b_x[c], b_rstd[0], b_coef], writes=[b_tmp[k]])
                P.op("act", lambda e, c=c, k=k: e.activation(
                    out=hc(c), in_=tmpf[:, k * T:(k + 1) * T], func=AF.Identity,
                    bias=coef[:, cofs + 16 + c:cofs + 17 + c], scale=1.0), reads=[b_tmp[k], b_coef], writes=[b_h[c]])

        def mm_std(src_ap, rhs_of, nk, rhs_bufs, ncols=None):
            s = load_piece(src_ap, nk * 128)
            b = next_bank()
            fns = []
            for kc in range(nk):
                fns.append(lambda e, s=s, kc=kc, b=b: e.matmul(
                    banks[b][:], slot(s)[:, kc * 128:(kc + 1) * 128], rhs_of(kc), start=(kc == 0), stop=(kc == nk - 1)))
            P.op("pe", fns, reads=[b_slot[s]] + rhs_bufs, writes=[bankb[b]])
            return b

        def residual(gofs):
            P.op("act", lambda e: e.activation(out=rstd[:, T:2 * T], in_=banks[STAT][:], func=AF.Sqrt, bias=epst[:, 0:1], scale=1.0),
                 reads=[bankb[STAT], b_const], writes=[b_rstd[1]])
            P.op("dve", lambda e: e.reciprocal(out=rstd[:, T:2 * T], in_=rstd[:, T:2 * T]), reads=[b_rstd[1]], writes=[b_rstd[1]])
            for c in range(KC):
                k = 2 + c % 2
                P.op("dve", lambda e, c=c, k=k: e.scalar_tensor_tensor(
                    out=tmpf[:, k * T:(k + 1) * T], in0=ybuf[:, c * T:(c + 1) * T], scalar=coef[:, gofs + c:gofs + c + 1],
                    in1=rstd[:, T:2 * T], op0=ALU.mult, op1=ALU.mult), reads=[b_y[c], b_rstd[1], b_coef], writes=[b_tmp[k]])
                P.op("dve", lambda e, c=c, k=k: e.tensor_tensor(out=xc(c), in0=xc(c), in1=tmpf[:, k * T:(k + 1) * T], op=ALU.add),
                     reads=[b_tmp[k], b_x[c]], writes=[b_x[c]])

        def out_proj_to_y(j, b):
            k = j % 2
            P.op("act", lambda e, j=j, b=b: e.activation(out=ybuf[:, j * T:(j + 1) * T], in_=banks[b][:], func=AF.Identity),
                 reads=[bankb[b]], writes=[b_y[j]])
            P.op("act", lambda e, b=b, k=k: e.activation(out=sq[:, k * T:(k + 1) * T], in_=banks[b][:], func=AF.Square),
                 reads=[bankb[b]], writes=[b_sq[k]])
            P.op("pe", lambda e, j=j, k=k: e.matmul(banks[STAT][:], onesD[:], sq[:, k * T:(k + 1) * T],
                                                      start=(j == 0), stop=(j == KC - 1)),
                 reads=[b_sq[k], b_const], writes=[bankb[STAT]])

        def dbg(name, ap, bufs, eng="sp"):
            if debug:
                P.dma(eng, lambda e: e.dma_start(out=dbg_out[name], in_=ap), sem_o, reads=bufs)

        for step in range(n_steps):
            for qd in range(4):
                P.dma("sp", lambda e, qd=qd, step=step: e.dma_start(
                    out=xres[:, qd * 2048:(qd + 1) * 2048], in_=xin[step][:, qd * 2048:(qd + 1) * 2048]),
                    sem_x, writes=b_x[qd * 4:(qd + 1) * 4])

            if fused and step >= 1:
                for hv in range(2):
                    P.dma("sp", lambda e, hv=hv: e.dma_start(out=ybuf[:, hv * 4096:(hv + 1) * 4096], in_=gath[hv][0:128, :]),
                          sem_r, reads=[b_gath[hv]], writes=b_y[hv * 8:(hv + 1) * 8])
                for c in range(KC):
                    P.op("dve", lambda e, c=c: e.scalar_tensor_tensor(
                        out=xc(c), in0=ybuf[:, c * T:(c + 1) * T], scalar=flagt[:, 0:1], in1=xc(c), op0=ALU.mult, op1=ALU.add),
                        reads=[b_y[c], b_x[c], b_const], writes=[b_x[c]])

            norm_in(0)
            dbg("d_h", hbuf[:].bitcast(F32), b_h)
            if step == 0:
                dbg("d_mod", modv[:], [b_mod])

            fns = []
            for kc in range(KC):
                fns.append(lambda e, kc=kc: e.matmul(banks[MISC][0:16, :], wlrt[:, kc * 16:(kc + 1) * 16], hc(kc),
                                                      start=(kc == 0), stop=(kc == KC - 1)))
            P.op("pe", fns, reads=b_h + [b_const], writes=[bankb[MISC]])
            P.op("act", lambda e: e.activation(out=lrT[0:16, :], in_=banks[MISC][0:16, :], func=AF.Identity),
                 reads=[bankb[MISC]], writes=[b_lrT])
            for tb in range(4):
                b = next_bank()
                P.op("pe", lambda e, tb=tb, b=b: e.matmul(banks[b][:], lrT[0:17, tb * 128:(tb + 1) * 128], wa2t[0:17, :],
                                                           start=True, stop=True),
                     reads=[b_lrT, b_const], writes=[bankb[b]])
                k = tb % 2
                P.op("act", lambda e, b=b, k=k: e.activation(out=tmpf[:, k * T:(k + 1) * T], in_=banks[b][:], func=AF.Exp, scale=-1.0),
                     reads=[bankb[b]], writes=[b_tmp[k]])
                P.op("act", lambda e, tb=tb, k=k: e.activation(out=laf[:, tb * T:(tb + 1) * T], in_=tmpf[:, k * T:(k + 1) * T],
                                                                 func=AF.Ln, bias=1.0, scale=1.0),
                     reads=[b_tmp[k]], writes=[b_la[tb]])
            dbg("d_la", laf, b_la)
            for tb in range(4):
                b = next_bank()
                P.op("pe", lambda e, tb=tb, b=b: e.matmul(banks[b][:], maskU[:], laf[:, tb * T:(tb + 1) * T], start=True, stop=True),
                     reads=[b_la[tb], b_const], writes=[bankb[b]])
                P.op("act", lambda e, tb=tb, b=b: e.activation(out=Ef[:, tb * T:(tb + 1) * T], in_=banks[b][:], func=AF.Exp, scale=-1.0 / 16),
                     reads=[bankb[b]], writes=[b_E])
            dbg("d_E", Ef, [b_E])
            fns = []
            for tb in range(4):
                for hd in range(4):
                    col = (tb * 4 + hd) * 2
                    fns.append(lambda e, tb=tb, hd=hd, col=col: e.matmul(
                        banks[MISC][:, col:col + 2], laf[:, tb * T + hd * 128: tb * T + (hd + 1) * 128], cindt[:],
                        start=True, stop=True))
            P.op("pe", fns, reads=b_la + [b_const], writes=[bankb[MISC]])
            P.op("act", lambda e: e.activation(out=decay[:], in_=banks[MISC][:, 0:32], func=AF.Exp, scale=-1.0 / 16),
                 reads=[bankb[MISC]], writes=[b_decay])
            dbg("d_dec", decay[:], [b_decay])

            def mm_tok(piece):
                s = load_piece(win[piece], 2048)
                b = next_bank()
                fns = []
                for tb in range(4):
                    for kc in range(KC):
                        fns.append(lambda e, s=s, b=b, tb=tb, kc=kc: e.matmul(
                            banks[b][:, tb * 128:(tb + 1) * 128], hbuf[:, kc * T + tb * 128: kc * T + (tb + 1) * 128],
                            slot(s)[:, kc * 128:(kc + 1) * 128], start=(kc == 0), stop=(kc == KC - 1)))
                P.op("pe", fns, reads=[b_slot[s]] + b_h, writes=[bankb[b]])
                return b

            kd3 = kdec.rearrange("p (tb c) -> p tb c", tb=4)
            E3 = Ef.rearrange("p (tb c) -> p tb c", tb=4)
            v3 = vtm.rearrange("p (tb c) -> p tb c", tb=4)
            for hd in range(4):
                b = mm_tok(PK + hd)
                P.op("dve", lambda e, b=b, hd=hd: e.tensor_tensor(
                    out=kd3[:, :, hd * 128:(hd + 1) * 128], in0=banks[b][:].rearrange("p (tb c) -> p tb c", tb=4),
                    in1=E3[:, :, hd * 128:(hd + 1) * 128], op=ALU.mult), reads=[bankb[b], b_E], writes=[b_kdec])
            for jv in range(8):
                b = mm_tok(PV + jv)
                P.op("act", lambda e, b=b, jv=jv: e.activation(
                    out=v3[:, :, jv * 128:(jv + 1) * 128], in_=banks[b][:].rearrange("p (tb c) -> p tb c", tb=4),
                    func=AF.Identity), reads=[bankb[b]], writes=[b_v])
            for hd in range(4):
                b = mm_std(win[PQ + hd], hc, KC, b_h)
                P.op("act", lambda e, b=b, hd=hd: e.activation(out=qT[:, hd * T:(hd + 1) * T], in_=banks[b][:], func=AF.Identity,
                                                                 scale=float(128 ** -0.5)), reads=[bankb[b]], writes=[b_q[hd]])
            dbg("d_q", qT.bitcast(F32), b_q)
            dbg("d_k", kdec.bitcast(F32), [b_kdec])
            dbg("d_v", vtm.bitcast(F32), [b_v])

            o3 = oT.rearrange("p (c t) -> p c t", c=8)
            for ci in range(8):
                tb, half = ci // 2, ci % 2
                r0 = half * 64
                for hd in range(4):
                    P.op("pe", lambda e, hd=hd, tb=tb, r0=r0: e.matmul(
                        banks[hd][:, 0:256], kdec[r0:r0 + 64, tb * 512 + hd * 128: tb * 512 + (hd + 1) * 128],
                        vtm[r0:r0 + 64, tb * 1024 + hd * 256: tb * 1024 + (hd + 1) * 256], start=True, stop=True),
                        reads=[b_kdec, b_v], writes=[bankb[hd]])
                for hd in range(4):
                    dcol = (tb * 4 + hd) * 2 + half
                    P.op("dve", lambda e, hd=hd, dcol=dcol: e.scalar_tensor_tensor(
                        out=S32[:, hd * 256:(hd + 1) * 256], in0=S32[:, hd * 256:(hd + 1) * 256], scalar=decay[:, dcol:dcol + 1],
                        in1=banks[hd][:, 0:256], op0=ALU.mult, op1=ALU.add), reads=[bankb[hd], b_decay, b_S[hd]], writes=[b_S[hd]])
                    kk = ci % 2
                    P.op("act", lambda e, hd=hd, kk=kk: e.activation(
                        out=Sb[:, (hd * 2 + kk) * 256:(hd * 2 + kk + 1) * 256], in_=S32[:, hd * 256:(hd + 1) * 256], func=AF.Identity),
                        reads=[b_S[hd]], writes=[b_Sb[hd][kk]])
                for hd in range(4):
                    kk = ci % 2
                    ob = 4 + hd
                    fns = []
                    for hf in range(2):
                        fns.append(lambda e, hd=hd, kk=kk, hf=hf, ob=ob, ci=ci: e.matmul(
                            banks[ob][:, hf * 64:(hf + 1) * 64],
                            Sb[:, (hd * 2 + kk) * 256 + hf * 128:(hd * 2 + kk) * 256 + (hf + 1) * 128],
                            qT[:, hd * T + ci * 64: hd * T + (ci + 1) * 64], start=True, stop=True))
                    P.op("pe", fns, reads=[b_Sb[hd][kk], b_q[hd]], writes=[bankb[ob]])
                    P.op("act", lambda e, hd=hd, ob=ob, ci=ci: e.activation(
                        out=o3[:, hd * 2:hd * 2 + 2, ci * 64:(ci + 1) * 64],
                        in_=banks[ob][:, 0:128].rearrange("p (h t) -> p h t", h=2), func=AF.Identity),
                        reads=[bankb[ob]], writes=[b_o[hd * 2], b_o[hd * 2 + 1]])
            dbg("d_o", oT, b_o)

            for jr in range(8):
                b = mm_std(win[PR + jr], hc, KC, b_h)
                P.op("act", lambda e, b=b, jr=jr: e.activation(out=rs[:, jr * T:(jr + 1) * T], in_=banks[b][:], func=AF.Silu),
                     reads=[bankb[b]], writes=[b_rs[jr]])
            for hd in range(4):
                for hf in range(2):
                    c = hd * 2 + hf
                    k = hf
                    P.op("act", lambda e, c=c, k=k: e.activation(out=sq[:, k * T:(k + 1) * T], in_=oT[:, c * T:(c + 1) * T], func=AF.Square),
                         reads=[b_o[c]], writes=[b_sq[k]])
                    P.op("pe", lambda e, k=k, hf=hf: e.matmul(banks[STAT][:], ones256[:], sq[:, k * T:(k + 1) * T],
                                                                start=(hf == 0), stop=(hf == 1)),
                         reads=[b_sq[k], b_const], writes=[bankb[STAT]])
                P.op("act", lambda e: e.activation(out=rstd[:, T:2 * T], in_=banks[STAT][:], func=AF.Sqrt, bias=epst[:, 0:1], scale=1.0),
                     reads=[bankb[STAT], b_const], writes=[b_rstd[1]])
                P.op("dve", lambda e: e.reciprocal(out=rstd[:, T:2 * T], in_=rstd[:, T:2 * T]), reads=[b_rstd[1]], writes=[b_rstd[1]])
                for hf in range(2):
                    c = hd * 2 + hf
                    k = 2 + hf
                    P.op("dve", lambda e, c=c, k=k, hf=hf: e.scalar_tensor_tensor(
                        out=tmpf[:, k * T:(k + 1) * T], in0=oT[:, c * T:(c + 1) * T], scalar=glagt[:, hf:hf + 1],
                        in1=rstd[:, T:2 * T], op0=ALU.mult, op1=ALU.mult), reads=[b_o[c], b_rstd[1], b_const], writes=[b_tmp[k]])
                    P.op("dve", lambda e, c=c, k=k: e.tensor_tensor(
                        out=gin[:, c * T:(c + 1) * T], in0=tmpf[:, k * T:(k + 1) * T], in1=rs[:, c * T:(c + 1) * T], op=ALU.mult),
                        reads=[b_tmp[k], b_rs[c]], writes=[b_gin[c]])
            dbg("d_gin", gin.bitcast(F32), b_gin)

            for c in range(8):
                k = c % 2
                ub = ubuf[:, k * (T + 2):(k + 1) * (T + 2)]
                cb_ = cbuf[:, k * T:(k + 1) * T]
                b1 = mm_std(win[PCC + c], hc, KC, b_h)
                P.op("act", lambda e, b1=b1, k=k: e.activation(out=tmpf[:, k * T:(k + 1) * T], in_=banks[b1][:], func=AF.Identity),
                     reads=[bankb[b1]], writes=[b_tmp[k]])
                b2 = mm_std(win[PCX + c], hc, KC, b_h)
                P.op("dve", lambda e, ub=ub, c=c: e.tensor_copy(out=ub[:, 0:2], in_=uhalo[:, 2 * c:2 * c + 2]),
                     reads=[b_uh[c]], writes=[b_ubuf[k]])
                P.op("dve", lambda e, ub=ub, b2=b2, k=k: e.tensor_tensor(out=ub[:, 2:T + 2], in0=banks[b2][:], in1=tmpf[:, k * T:(k + 1) * T], op=ALU.mult),
                     reads=[bankb[b2], b_tmp[k], b_ubuf[k]], writes=[b_ubuf[k]])
                P.op("dve", lambda e, ub=ub, c=c: e.tensor_copy(out=uhalo[:, 2 * c:2 * c + 2], in_=ub[:, T:T + 2]),
                     reads=[b_ubuf[k]], writes=[b_uh[c]])
                P.op("dve", lambda e, ub=ub, cb_=cb_, c=c: e.tensor_scalar(out=cb_, in0=ub[:, 0:T], scalar1=cmwt[:, 3 * c:3 * c + 1], scalar2=None, op0=ALU.mult),
                     reads=[b_ubuf[k], b_const], writes=[b_cbuf[k]])
                P.op("dve", lambda e, ub=ub, cb_=cb_, c=c: e.scalar_tensor_tensor(out=cb_, in0=ub[:, 1:T + 1], scalar=cmwt[:, 3 * c + 1:3 * c + 2], in1=cb_, op0=ALU.mult, op1=ALU.add),
                     reads=[b_ubuf[k], b_cbuf[k]], writes=[b_cbuf[k]])
                P.op("dve", lambda e, ub=ub, cb_=cb_, c=c: e.scalar_tensor_tensor(out=cb_, in0=ub[:, 2:T + 2], scalar=cmwt[:, 3 * c + 2:3 * c + 3], in1=cb_, op0=ALU.mult, op1=ALU.add),
                     reads=[b_ubuf[k], b_cbuf[k]], writes=[b_cbuf[k]])
                b3 = mm_std(win[PCB + c], hc, KC, b_h)
                P.op("dve", lambda e, b3=b3, cb_=cb_, c=c: e.tensor_tensor(out=cvb[:, c * T:(c + 1) * T], in0=banks[b3][:], in1=cb_, op=ALU.mult),
                     reads=[bankb[b3], b_cbuf[k]], writes=[b_cvb[c]])
            dbg("d_cvb", cvb.bitcast(F32), b_cvb)

            for j in range(16):
                ka, kb = (j % 2) * 2, (j % 2) * 2 + 1
                ta = tmpf[:, ka * T:(ka + 1) * T]
                tbv = tmpf[:, kb * T:(kb + 1) * T]
                b1 = mm_std(win[PGA + j], hc, KC, b_h)
                P.op("act", lambda e, b1=b1, ta=ta: e.activation(out=ta, in_=banks[b1][:], func=AF.Sigmoid),
                     reads=[bankb[b1]], writes=[b_tmp[ka]])
                b2 = mm_std(woutg[j], lambda kc: gin[:, kc * T:(kc + 1) * T], 8, b_gin)
                P.op("dve", lambda e, b2=b2, ta=ta: e.tensor_tensor(out=ta, in0=banks[b2][:], in1=ta, op=ALU.mult),
                     reads=[bankb[b2], b_tmp[ka]], writes=[b_tmp[ka]])
                b3 = mm_std(win[PGB + j], hc, KC, b_h)
                P.op("act", lambda e, b3=b3, tbv=tbv: e.activation(out=tbv, in_=banks[b3][:], func=AF.Sigmoid),
                     reads=[bankb[b3]], writes=[b_tmp[kb]])
                b4 = mm_std(woutc[j], lambda kc: cvb[:, kc * T:(kc + 1) * T], 8, b_cvb)
                P.op("dve", lambda e, b4=b4, tbv=tbv: e.tensor_tensor(out=tbv, in0=banks[b4][:], in1=tbv, op=ALU.mult),
                     reads=[bankb[b4], b_tmp[kb]], writes=[b_tmp[kb]])
                P.op("dve", lambda e, j=j, ta=ta, tbv=tbv: e.tensor_tensor(out=mbuf[:, j * T:(j + 1) * T], in0=ta, in1=tbv, op=ALU.add),
                     reads=[b_tmp[ka], b_tmp[kb]], writes=[b_m[j]])
            dbg("d_m", mbuf.bitcast(F32), b_m)

            for j in range(16):
                b = mm_std(wo[j], lambda kc: mbuf[:, kc * T:(kc + 1) * T], KC, b_m)
                out_proj_to_y(j, b)
            residual(32)
            dbg("d_xmid", xres[:], b_x)

            norm_in(48)
            for j in range(FC):
                k = j % 2
                ub = ubuf[:, k * (T + 2):(k + 1) * (T + 2)]
                cb_ = cbuf[:, k * T:(k + 1) * T]
                b1 = mm_std(wup[j], hc, KC, b_h)
                P.op("dve", lambda e, ub=ub, j=j: e.tensor_copy(out=ub[:, 0:2], in_=ghalo[:, 2 * j:2 * j + 2]),
                     reads=[b_gh[j]], writes=[b_ubuf[k]])
                P.op("act", lambda e, ub=ub, b1=b1: e.activation(out=ub[:, 2:T + 2], in_=banks[b1][:], func=AF.Identity),
                     reads=[bankb[b1], b_ubuf[k]], writes=[b_ubuf[k]])
                P.op("dve", lambda e, ub=ub, j=j: e.tensor_copy(out=ghalo[:, 2 * j:2 * j + 2], in_=ub[:, T:T + 2]),
                     reads=[b_ubuf[k]], writes=[b_gh[j]])
                P.op("dve", lambda e, ub=ub, cb_=cb_, j=j: e.tensor_scalar(out=cb_, in0=ub[:, 0:T], scalar1=fcwt[:, 3 * j:3 * j + 1], scalar2=None, op0=ALU.mult),
                     reads=[b_ubuf[k], b_const], writes=[b_cbuf[k]])
                P.op("dve", lambda e, ub=ub, cb_=cb_, j=j: e.scalar_tensor_tensor(out=cb_, in0=ub[:, 1:T + 1], scalar=fcwt[:, 3 * j + 1:3 * j + 2], in1=cb_, op0=ALU.mult, op1=ALU.add),
                     reads=[b_ubuf[k], b_cbuf[k]], writes=[b_cbuf[k]])
                P.op("dve", lambda e, ub=ub, cb_=cb_, j=j: e.scalar_tensor_tensor(out=cb_, in0=ub[:, 2:T + 2], scalar=fcwt[:, 3 * j + 2:3 * j + 3], in1=cb_, op0=ALU.mult, op1=ALU.add),
                     reads=[b_ubuf[k], b_cbuf[k]], writes=[b_cbuf[k]])
                P.op("act", lambda e, cb_=cb_, k=k: e.activation(out=tmpf[:, k * T:(k + 1) * T], in_=cb_, func=AF.Gelu),
                     reads=[b_cbuf[k]], writes=[b_tmp[k]])
                b2 = mm_std(wup[FC + j], hc, KC, b_h)
                P.op("dve", lambda e, b2=b2, j=j, k=k: e.tensor_tensor(out=actb[:, j * T:(j + 1) * T], in0=banks[b2][:], in1=tmpf[:, k * T:(k + 1) * T], op=ALU.mult),
                     reads=[bankb[b2], b_tmp[k]], writes=[b_act[j]])
            for j in range(16):
                b = next_bank()
                for g in range(4):
                    s = load_piece(wdn[j * 4 + g], 1408)
                    fns = []
                    for kk in range(11):
                        kc = g * 11 + kk
                        fns.append(lambda e, s=s, kk=kk, kc=kc, b=b: e.matmul(
                            banks[b][:], slot(s)[:, kk * 128:(kk + 1) * 128], actb[:, kc * T:(kc + 1) * T],
                            start=(kc == 0), stop=(kc == FC - 1)))
                    P.op("pe", fns, reads=[b_slot[s]] + b_act[g * 11:(g + 1) * 11], writes=[bankb[b]])
                out_proj_to_y(j, b)
            residual(80)

            if fused and step < n_steps - 1:
                for hv in range(2):
                    P.dma("sp", lambda e, hv=hv: e.dma_start(out=outbox[hv][:, :], in_=xres[:, hv * 4096:(hv + 1) * 4096]),
                          sem_b, reads=b_x[hv * 8:(hv + 1) * 8], writes=[b_outbox[hv]])
                prev = None
                for hv in range(2):
                    prev = P.dma("pool", lambda e, hv=hv: e.collective_compute(
                        "AllGather", ALU.bypass, replica_groups=[[0, 1], [2, 3], [4, 5], [6, 7]],
                        ins=[outbox[hv].ap().opt()], outs=[gath[hv].ap().opt()]),
                        sem_cc, reads=[b_outbox[hv]], writes=[b_gath[hv]], deps=[prev], inc=1)
            for qd in range(4):
                P.dma("sp", lambda e, qd=qd, step=step: e.dma_start(
                    out=xout[step][:, qd * 2048:(qd + 1) * 2048], in_=xres[:, qd * 2048:(qd + 1) * 2048]),
                    sem_o, reads=b_x[qd * 4:(qd + 1) * 4])
            if fused and step == 0:
                P.op("dve", lambda e: e.tensor_scalar(out=S32[:], in0=S32[:], scalar1=flagt[:, 1:2], scalar2=None, op0=ALU.mult),
                     reads=b_S + [b_const], writes=b_S)
                P.op("dve", lambda e: e.tensor_scalar(out=uhalo[:], in0=uhalo[:], scalar1=flagt[:, 1:2], scalar2=None, op0=ALU.mult),
                     reads=b_uh + [b_const], writes=b_uh)
                P.op("dve", lambda e: e.tensor_scalar(out=ghalo[:], in0=ghalo[:], scalar1=flagt[:, 1:2], scalar2=None, op0=ALU.mult),
                     reads=b_gh + [b_const], writes=b_gh)

        P.dma("sp", lambda e: e.dma_start(out=s_out, in_=S32[:]), sem_o, reads=b_S)
        P.dma("sp", lambda e: e.dma_start(out=uh_out, in_=uhalo[:]), sem_o, reads=b_uh)
        last = P.dma("sp", lambda e: e.dma_start(out=gh_out, in_=ghalo[:]), sem_o, reads=b_gh)

        print("sbuf bytes remaining", nc.sbuf_bytes_remaining)
        P.finalize(engsem)
        final_o = P.dma_counts[id(sem_o)]

        block = E(nc.Block())

        @block.tensor
        def _(e):
            P.replay("pe", e)

        @block.scalar
        def _(e):
            P.replay("act", e)

        @block.vector
        def _(e):
            P.replay("dve", e)

        @block.gpsimd
        def _(e):
            P.replay("pool", e)

        @block.sync
        def _(e):
            P.replay("sp", e)
            e.wait_ge(sem_o, final_o)

    return nc


def _pieces(w, col0s, nk):
    out = np.empty((len(col0s), 128, nk * 128), np.float32)
    wk = w.reshape(nk, 128, w.shape[1])
    for j, c0 in enumerate(col0s):
        out[j] = wk[:, :, c0:c0 + 128].transpose(1, 0, 2).reshape(128, nk * 128)
    return out


def _fm(v):
    return np.ascontiguousarray(v.reshape(-1, 128).T)


_MASKU = None


def _consts():
    global _MASKU
    if _MASKU is None:
        j = np.arange(128)[:, None]
        l = np.arange(128)[None, :]
        mu = ((j > l) & ((j // 64) == (l // 64))).astype(np.float32)
        ci = np.zeros((128, 2), np.float32)
        ci[:64, 0] = 1.0
        ci[64:, 1] = 1.0
        _MASKU = (mu, ci)
    return _MASKU


def layer_maps(l, w_ada, b_ada, norm_g, w_in, w_a2, b_a2, gla_norm_g, w_out_gla, conv_mix_w,
               w_out_conv, w_o, w_up, ffn_conv_w, w_down):
    col0 = [j * 128 for j in range(24)] + [3088 + j * 128 for j in range(56)]
    mu, ci = _consts()
    m = {}
    m["wada"] = _pieces(w_ada[l], [j * 128 for j in range(96)], 16)
    m["bada"] = _fm(b_ada[l])
    m["ng"] = np.ascontiguousarray(np.concatenate([_fm(norm_g[l, i]) for i in range(4)], axis=1))
    m["win"] = _pieces(w_in[l], col0, 16)
    m["wlr"] = np.ascontiguousarray(w_in[l][:, 3072:3088].reshape(16, 128, 16).transpose(1, 0, 2).reshape(128, 256))
    m["wa2e"] = np.ascontiguousarray(np.concatenate([w_a2[l], b_a2[l][None, :]], axis=0))
    m["glag"] = _fm(gla_norm_g[l])
    m["woutg"] = _pieces(w_out_gla[l], [j * 128 for j in range(16)], 8)
    m["woutc"] = _pieces(w_out_conv[l], [j * 128 for j in range(16)], 8)
    m["cmw"] = np.ascontiguousarray(conv_mix_w[l].reshape(3, 8, 128).transpose(2, 1, 0).reshape(128, 24))
    m["wo"] = _pieces(w_o[l], [j * 128 for j in range(16)], 16)
    m["wup"] = _pieces(w_up[l], [j * 128 for j in range(88)], 16)
    m["fcw"] = np.ascontiguousarray(ffn_conv_w[l].reshape(3, FC, 128).transpose(2, 1, 0).reshape(128, 132))
    wd = _pieces(w_down[l], [j * 128 for j in range(16)], FC)
    m["wdn"] = np.ascontiguousarray(wd.reshape(16, 128, 4, 1408).transpose(0, 2, 1, 3).reshape(64, 128, 1408))
    m["masku"] = mu
    m["cind"] = ci
    return m


def x_tiles(xb):
    s = xb.shape[0]
    return np.ascontiguousarray(xb.reshape(s // T, T, KC, 128).transpose(0, 3, 2, 1).reshape(s // T, 128, KC * T))


def x_untile(t):
    n = t.shape[0]
    return np.ascontiguousarray(t.reshape(n, 128, KC, T).transpose(0, 3, 2, 1).reshape(n * T, D))


_NC_CACHE = {}


def _get_nc():
    if "nc" not in _NC_CACHE:
        _NC_CACHE["nc"] = build(NTILE + 1, fused=True)
    return _NC_CACHE["nc"]


def kernel(x, c, w_ada, b_ada, norm_g, w_in, w_a2, b_a2, gla_norm_g, w_out_gla,
           conv_mix_w, w_out_conv, w_o, w_up, ffn_conv_w, w_down):
    args = [np.asarray(a, np.float32) for a in (w_ada, b_ada, norm_g, w_in, w_a2, b_a2, gla_norm_g, w_out_gla,
                                                conv_mix_w, w_out_conv, w_o, w_up, ffn_conv_w, w_down)]
    x = np.asarray(x, np.float32)
    c = np.asarray(c, np.float32)
    B = x.shape[0]
    nc = _get_nc()
    lm = [layer_maps(l, *args) for l in range(2)]
    zeros_x = np.zeros((NTILE + 1, 128, KC * T), np.float32)
    fl = [np.tile(np.array([[0.0, 1.0]], np.float32), (128, 1)), np.tile(np.array([[1.0, 0.0]], np.float32), (128, 1))]
    in_maps = []
    for core in range(2 * B):
        b, l = core // 2, core % 2
        m = dict(lm[l])
        m["cvec"] = _fm(c[b])
        if l == 0:
            m["xin"] = np.concatenate([x_tiles(x[b]), zeros_x[:1]], axis=0)
        else:
            m["xin"] = zeros_x
        m["flags"] = fl[l]
        m["s_in"] = np.zeros((128, 1024), np.float32)
        m["uh_in"] = np.zeros((128, 16), np.float32)
        m["gh_in"] = np.zeros((128, 88), np.float32)
        in_maps.append(m)
    res = run_bass_kernel_spmd(nc, in_maps, core_ids=list(range(2 * B)))
    out = np.stack([x_untile(np.asarray(res.results[2 * b + 1]["xout"])[1:]) for b in range(B)], axis=0)
    return out.astype(np.float32)
```

```python
import numpy as np
from contextlib import ExitStack
import concourse.bass as bass
import concourse.mybir as mybir
from concourse.bass_utils import run_bass_kernel_spmd

F32 = mybir.dt.float32
BF16 = mybir.dt.bfloat16
AF = mybir.ActivationFunctionType
ALU = mybir.AluOpType

D = 2048
KC = 16
T = 512
NTILE = 8
DFF = 5632
FC = 44
EPS = 1e-6
NSLOT = 6
SLOTW = 2048

PQ, PK, PV, PR, PCB, PCC, PCX, PGA, PGB = 0, 4, 8, 16, 24, 32, 40, 48, 64


class Op:
    __slots__ = ("eng", "fns", "deps", "signal", "val", "dma_sem", "dma_val", "inc")

    def __init__(self, eng, fns, deps):
        self.eng = eng
        self.fns = fns
        self.deps = deps
        self.signal = False
        self.val = None
        self.dma_sem = None
        self.dma_val = None
        self.inc = 16


class Buf:
    __slots__ = ("w", "r", "lo", "hi", "psum", "last")

    def __init__(self, lo=None, hi=None, psum=False):
        self.w = None
        self.r = []
        self.lo = lo
        self.hi = hi
        self.psum = psum
        self.last = None


class Prog:
    def __init__(self):
        self.ops = {"pe": [], "act": [], "dve": [], "pool": [], "sp": []}
        self.arena_bufs = []
        self.dma_counts = {}

    def abuf(self, lo, hi):
        b = Buf(lo, hi)
        self.arena_bufs.append(b)
        return b

    def _deps(self, eng, reads, writes):
        deps = []
        for b in reads:
            if b.psum:
                if b.last is not None and b.last.eng != eng:
                    deps.append(b.last)
            elif b.w is not None:
                deps.append(b.w)
        for b in writes:
            if b.psum:
                if b.last is not None and b.last.eng != eng:
                    deps.append(b.last)
                continue
            if b.w is not None:
                deps.append(b.w)
            deps.extend(b.r)
            if b.lo is not None:
                for o in self.arena_bufs:
                    if o is not b and o.lo < b.hi and b.lo < o.hi:
                        if o.w is not None:
                            deps.append(o.w)
                        deps.extend(o.r)
        return deps

    def _note(self, op, reads, writes):
        for b in reads:
            if b.psum:
                b.last = op
            else:
                b.r.append(op)
        for b in writes:
            if b.psum:
                b.last = op
                continue
            b.w = op
            b.r = []
            if b.lo is not None:
                for o in self.arena_bufs:
                    if o is not b and o.lo < b.hi and b.lo < o.hi:
                        o.w = None
                        o.r = []

    def op(self, eng, fns, reads=(), writes=(), deps=()):
        if not isinstance(fns, (list, tuple)):
            fns = [fns]
        d = self._deps(eng, reads, writes) + [x for x in deps if x is not None]
        o = Op(eng, list(fns), d)
        self.ops[eng].append(o)
        self._note(o, reads, writes)
        return o

    def dma(self, eng, fn, sem, reads=(), writes=(), deps=(), inc=16):
        o = self.op(eng, fn, reads, writes, deps)
        c = self.dma_counts.get(id(sem), 0) + inc
        self.dma_counts[id(sem)] = c
        o.dma_sem = sem
        o.dma_val = c
        o.inc = inc
        return o

    def finalize(self, engsem):
        for lst in self.ops.values():
            for o in lst:
                for dd in o.deps:
                    if dd.dma_sem is None:
                        dd.signal = True
        for eng, lst in self.ops.items():
            c = 0
            for o in lst:
                if o.signal:
                    c += 1
                    o.val = c
        self.engsem = engsem

    def replay(self, eng, e):
        seen = {}
        for o in self.ops[eng]:
            for dd in o.deps:
                if dd.dma_sem is not None:
                    sem, val = dd.dma_sem, dd.dma_val
                else:
                    if dd.eng == eng and False:
                        continue
                    sem, val = self.engsem[dd.eng], dd.val
                k = id(sem)
                if seen.get(k, 0) >= val:
                    continue
                seen[k] = val
                e.wait_ge(sem, val)
            ins = None
            for f in o.fns:
                ins = f(e)
            if o.dma_sem is not None:
                if o.inc == 16:
                    ins.then_inc(o.dma_sem, 16)
                else:
                    ins.then_inc(o.dma_sem)
            elif o.signal:
                ins.then_inc(self.engsem[eng], 1)


def build(n_steps=1, debug=False, fused=False):
    nc = bass.Bass("TRN2", target_bir_lowering=False)

    def din(name, shape):
        return nc.dram_tensor(name, list(shape), F32, kind="ExternalInput").ap()

    def dout(name, shape):
        return nc.dram_tensor(name, list(shape), F32, kind="ExternalOutput").ap()

    xin = din("xin", [n_steps, 128, KC * T])
    cvec = din("cvec", [128, KC])
    wada = din("wada", [96, 128, 2048])
    bada = din("bada", [128, 96])
    ngd = din("ng", [128, 64])
    win = din("win", [80, 128, 2048])
    wlr = din("wlr", [128, 256])
    wa2e = din("wa2e", [17, 512])
    glag = din("glag", [128, 2])
    woutg = din("woutg", [16, 128, 1024])
    woutc = din("woutc", [16, 128, 1024])
    cmw = din("cmw", [128, 24])
    wo = din("wo", [16, 128, 2048])
    wup = din("wup", [88, 128, 2048])
    fcw = din("fcw", [128, 132])
    wdn = din("wdn", [64, 128, 1408])
    masku = din("masku", [128, 128])
    cind = din("cind", [128, 2])
    s_in = din("s_in", [128, 1024])
    uh_in = din("uh_in", [128, 16])
    gh_in = din("gh_in", [128, 88])
    flags = din("flags", [128, 2])
    if fused:
        outbox = [nc.dram_tensor(f"outbox{i}", [128, 4096], F32) for i in range(2)]
        gath = [nc.dram_tensor(f"gath{i}", [256, 4096], F32) for i in range(2)]
    xout = dout("xout", [n_steps, 128, KC * T])
    s_out = dout("s_out", [128, 1024])
    uh_out = dout("uh_out", [128, 16])
    gh_out = dout("gh_out", [128, 88])
    dbg_out = {}
    if debug:
        for nm, w in [("d_h", KC * T // 2), ("d_la", 2048), ("d_E", 2048), ("d_dec", 32), ("d_q", 1024),
                      ("d_k", 1024), ("d_v", 2048), ("d_o", 4096), ("d_gin", 2048), ("d_cvb", 2048),
                      ("d_m", 4096), ("d_xmid", KC * T), ("d_mod", 96)]:
            dbg_out[nm] = dout(nm, [128, w])

    P = Prog()
    with ExitStack() as es:
        E = es.enter_context

        def sb(name, shape, dt):
            return E(nc.sbuf_tensor(name, list(shape), dt))

        xres = sb("xres", [128, KC * T], F32)
        hbuf = sb("hbuf", [128, KC * T], BF16)
        arena = sb("arena", [128, 40960], BF16)
        ring = sb("ring", [128, NSLOT * SLOTW], BF16)
        sq = sb("sq", [128, 2 * T], BF16)
        rstd = sb("rstd", [128, 2 * T], F32)
        tmpf = sb("tmpf", [128, 4 * T], F32)
        ubuf = sb("ubuf", [128, 2 * (T + 2)], F32)
        cbuf = sb("cbuf", [128, 2 * T], F32)
        S32 = sb("S32", [128, 1024], F32)
        Sb = sb("Sb", [128, 2048], BF16)
        lrT = sb("lrT", [32, T], BF16)
        onesD = sb("onesD", [128, 128], BF16)
        ones256 = sb("ones256", [128, 128], BF16)
        maskU = sb("maskU", [128, 128], F32)
        cindt = sb("cindt", [128, 2], F32)
        wa2t = sb("wa2t", [32, 512], BF16)
        wlrt = sb("wlrt", [128, 256], BF16)
        modv = sb("modv", [128, 96], F32)
        badat = sb("badat", [128, 96], F32)
        ngt = sb("ngt", [128, 64], F32)
        coef = sb("coef", [128, 96], F32)
        glagt = sb("glagt", [128, 2], F32)
        cmwt = sb("cmwt", [128, 24], F32)
        fcwt = sb("fcwt", [128, 132], F32)
        cst = sb("cst", [128, KC], F32)
        csb = sb("csb", [128, KC], BF16)
        decay = sb("decay", [128, 32], F32)
        uhalo = sb("uhalo", [128, 16], F32)
        ghalo = sb("ghalo", [128, 88], F32)
        epst = sb("epst", [128, 1], F32)
        flagt = sb("flagt", [128, 2], F32)

        banks = [E(nc.psum_tensor(f"bank{i}", [128, 512], F32)) for i in range(8)]
        bankb = [Buf(psum=True) for _ in range(8)]

        engsem = {k: E(nc.semaphore(f"sem_{k}")) for k in ["pe", "act", "dve", "pool", "sp"]}
        slot_sem = [E(nc.semaphore(f"slot{i}")) for i in range(NSLOT)]
        sem_x = E(nc.semaphore("sem_x"))
        sem_c = E(nc.semaphore("sem_c"))
        sem_o = E(nc.semaphore("sem_o"))
        sem_b = E(nc.semaphore("sem_b"))
        sem_cc = E(nc.semaphore("sem_cc"))
        sem_r = E(nc.semaphore("sem_r"))
        b_outbox = [Buf(), Buf()]
        b_gath = [Buf(), Buf()]

        A0, B0 = 0, 16384

        def aview(lo, n, dt=BF16):
            ap = arena[:, lo:lo + n]
            return ap.bitcast(F32) if dt == F32 else ap

        qT = aview(A0 + 0, 2048)
        kdec = aview(A0 + 2048, 2048)
        vtm = aview(A0 + 4096, 4096)
        Ef = aview(A0 + 8192, 4096, F32)
        laf = aview(A0 + 12288, 4096, F32)
        oT = aview(A0 + 8192, 8192, F32)
        ybuf = aview(A0, 16384, F32)
        rs = aview(B0 + 0, 4096)
        gin = aview(B0 + 4096, 4096)
        cvb = aview(B0 + 8192, 4096)
        mbuf = aview(B0 + 12288, 8192)
        actb = aview(B0, FC * T)

        def ab(lo, n):
            return P.abuf(lo, lo + n)

        b_q = [ab(A0 + h * 512, 512) for h in range(4)]
        b_kdec = ab(A0 + 2048, 2048)
        b_v = ab(A0 + 4096, 4096)
        b_E = ab(A0 + 8192, 4096)
        b_la = [ab(A0 + 12288 + tb * 1024, 1024) for tb in range(4)]
        b_o = [ab(A0 + 8192 + c * 1024, 1024) for c in range(8)]
        b_y = [ab(A0 + c * 1024, 1024) for c in range(16)]
        b_rs = [ab(B0 + c * 512, 512) for c in range(8)]
        b_gin = [ab(B0 + 4096 + c * 512, 512) for c in range(8)]
        b_cvb = [ab(B0 + 8192 + c * 512, 512) for c in range(8)]
        b_m = [ab(B0 + 12288 + c * 512, 512) for c in range(16)]
        b_act = [ab(B0 + c * 512, 512) for c in range(FC)]

        b_x = [Buf() for _ in range(KC)]
        b_h = [Buf() for _ in range(KC)]
        b_slot = [Buf() for _ in range(NSLOT)]
        b_sq = [Buf(), Buf()]
        b_rstd = [Buf(), Buf()]
        b_tmp = [Buf() for _ in range(4)]
        b_ubuf = [Buf(), Buf()]
        b_cbuf = [Buf(), Buf()]
        b_S = [Buf() for _ in range(4)]
        b_Sb = [Buf(), Buf()]
        b_lrT = Buf()
        b_const = Buf()
        b_mod = Buf()
        b_coef = Buf()
        b_decay = Buf()
        b_uh = [Buf() for _ in range(8)]
        b_gh = [Buf() for _ in range(FC)]

        def xc(c):
            return xres[:, c * T:(c + 1) * T]

        def hc(c):
            return hbuf[:, c * T:(c + 1) * T]

        def slot(i, n=SLOTW):
            return ring[:, i * SLOTW:i * SLOTW + n]

        piece_ctr = [0]

        def load_piece(src_ap, ncols):
            i = piece_ctr[0]
            piece_ctr[0] += 1
            s = i % NSLOT
            P.dma("pool", lambda e, s=s, src_ap=src_ap, ncols=ncols: e.dma_start(out=slot(s, ncols), in_=src_ap),
                  slot_sem[s], writes=[b_slot[s]])
            return s

        def cdma(dst, src, eng="sp", sem=None):
            return P.dma(eng, lambda e: e.dma_start(out=dst, in_=src), sem or sem_c, writes=[b_const])

        cdma(cst[:], cvec)
        cdma(badat[:], bada)
        cdma(ngt[:], ngd)
        cdma(glagt[:], glag)
        cdma(cmwt[:], cmw)
        cdma(fcwt[:], fcw)
        cdma(maskU[:], masku)
        cdma(cindt[:], cind)
        cdma(flagt[:], flags)
        P.dma("sp", lambda e: e.dma_start(out=S32[:], in_=s_in), sem_c, writes=b_S + [b_const])
        P.dma("sp", lambda e: e.dma_start(out=uhalo[:], in_=uh_in), sem_c, writes=b_uh + [b_const])
        P.dma("sp", lambda e: e.dma_start(out=ghalo[:], in_=gh_in), sem_c, writes=b_gh + [b_const])
        P.dma("pool", lambda e: e.dma_start(out=wlrt[:], in_=wlr), sem_c, writes=[b_const])
        P.dma("pool", lambda e: e.dma_start(out=wa2t[0:17, :], in_=wa2e), sem_c, writes=[b_const])

        P.op("dve", lambda e: e.memset(onesD[:], 1.0 / D), writes=[b_const])
        P.op("dve", lambda e: e.memset(ones256[:], 1.0 / 256), writes=[b_const])
        P.op("dve", lambda e: e.memset(lrT[:], 1.0), writes=[b_lrT])
        P.op("dve", lambda e: e.memset(epst[:], EPS), writes=[b_const])

        P.op("act", lambda e: e.activation(out=csb[:], in_=cst[:], func=AF.Silu), reads=[b_const], writes=[b_mod])
        MISC = 5
        b_modp = [Buf() for _ in range(4)]
        b_coefp = [Buf() for _ in range(4)]

        def mod_part(part):
            lo, hi = [(0, 32), (32, 48), (48, 80), (80, 96)][part]
            for j in range(lo, hi):
                s_ = load_piece(wada[j], 2048)
                fns = []
                for kc in range(KC):
                    fns.append(lambda e, s_=s_, kc=kc, j=j: e.matmul(
                        banks[MISC][:, j:j + 1], slot(s_)[:, kc * 128:(kc + 1) * 128], csb[:, kc:kc + 1],
                        start=(kc == 0), stop=(kc == KC - 1)))
                P.op("pe", fns, reads=[b_slot[s_], b_mod], writes=[bankb[MISC]])
            P.op("dve", lambda e: e.tensor_tensor(out=modv[:, lo:hi], in0=banks[MISC][:, lo:hi], in1=badat[:, lo:hi], op=ALU.add),
                 reads=[bankb[MISC], b_const], writes=[b_modp[part]])
            bm, bc = b_modp[part], b_coefp[part]
            if part == 0:
                P.op("dve", lambda e: e.tensor_scalar(out=coef[:, 0:16], in0=modv[:, 16:32], scalar1=1.0, scalar2=None, op0=ALU.add),
                     reads=[bm], writes=[bc])
                P.op("dve", lambda e: e.tensor_tensor(out=coef[:, 0:16], in0=coef[:, 0:16], in1=ngt[:, 0:16], op=ALU.mult),
                     reads=[bc, b_const], writes=[bc])
                P.op("dve", lambda e: e.tensor_copy(out=coef[:, 16:32], in_=modv[:, 0:16]), reads=[bm, bc], writes=[bc])
            elif part == 1:
                P.op("dve", lambda e: e.tensor_tensor(out=coef[:, 32:48], in0=modv[:, 32:48], in1=ngt[:, 16:32], op=ALU.mult),
                     reads=[bm, b_const], writes=[bc])
            elif part == 2:
                P.op("dve", lambda e: e.tensor_scalar(out=coef[:, 48:64], in0=modv[:, 64:80], scalar1=1.0, scalar2=None, op0=ALU.add),
                     reads=[bm], writes=[bc])
                P.op("dve", lambda e: e.tensor_tensor(out=coef[:, 48:64], in0=coef[:, 48:64], in1=ngt[:, 32:48], op=ALU.mult),
                     reads=[bc, b_const], writes=[bc])
                P.op("dve", lambda e: e.tensor_copy(out=coef[:, 64:80], in_=modv[:, 48:64]), reads=[bm, bc], writes=[bc])
            else:
                P.op("dve", lambda e: e.tensor_tensor(out=coef[:, 80:96], in0=modv[:, 80:96], in1=ngt[:, 48:64], op=ALU.mult),
                     reads=[bm, b_const], writes=[bc])

        mod_part(0)

        mmring = [0]

        def next_bank():
            b = mmring[0] % 4
            mmring[0] += 1
            return b

        STAT = 4

        def norm_in(cofs, b_coef, first_kc=None):
            for c in range(KC):
                k = c % 2
                P.op("act", lambda e, c=c, k=k: e.activation(out=sq[:, k * T:(k + 1) * T], in_=xc(c), func=AF.Square),
                     reads=[b_x[c]], writes=[b_sq[k]])
                P.op("pe", lambda e, c=c, k=k: e.matmul(banks[STAT][:], onesD[:], sq[:, k * T:(k + 1) * T],
                                                          start=(c == 0), stop=(c == KC - 1)),
                     reads=[b_sq[k], b_const], writes=[bankb[STAT]])
            P.op("act", lambda e: e.activation(out=rstd[:, 0:T], in_=banks[STAT][:], func=AF.Sqrt, bias=epst[:, 0:1], scale=1.0),
                 reads=[bankb[STAT], b_const], writes=[b_rstd[0]])
            P.op("dve", lambda e: e.reciprocal(out=rstd[:, 0:T], in_=rstd[:, 0:T]), reads=[b_rstd[0]], writes=[b_rstd[0]])
            for c in range(KC):
                k = c % 2
                P.op("dve", lambda e, c=c, k=k: e.scalar_tensor_tensor(
                    out=tmpf[:, k * T:(k + 1) * T], in0=xc(c), scalar=coef[:, cofs + c:cofs + c + 1], in1=rstd[:, 0:T],
                    op0=ALU.mult, op1=ALU.mult), reads=[b_x[c], b_rstd[0], b_coef], writes=[b_tmp[k]])
                P.op("act", lambda e, c=c, k=k: e.activation(
                    out=hc(c), in_=tmpf[:, k * T:(k + 1) * T], func=AF.Identity,
                    bias=coef[:, cofs + 16 + c:cofs + 17 + c], scale=1.0), reads=[b_tmp[k], b_coef], writes=[b_h[c]])

        def mm_std(src_ap, rhs_of, nk, rhs_bufs, per_kc=False):
            s = load_piece(src_ap, nk * 128)
            b = next_bank()
            fns = []
            for kc in range(nk):
                fns.append(lambda e, s=s, kc=kc, b=b: e.matmul(
                    banks[b][:], slot(s)[:, kc * 128:(kc + 1) * 128], rhs_of(kc), start=(kc == 0), stop=(kc == nk - 1)))
            if per_kc:
                for kc in range(nk):
                    P.op("pe", fns[kc], reads=[b_slot[s], rhs_bufs[kc]], writes=[bankb[b]])
            else:
                P.op("pe", fns, reads=[b_slot[s]] + rhs_bufs, writes=[bankb[b]])
            return b

        def residual(gofs, b_coef):
            P.op("act", lambda e: e.activation(out=rstd[:, T:2 * T], in_=banks[STAT][:], func=AF.Sqrt, bias=epst[:, 0:1], scale=1.0),
                 reads=[bankb[STAT], b_const], writes=[b_rstd[1]])
            P.op("dve", lambda e: e.reciprocal(out=rstd[:, T:2 * T], in_=rstd[:, T:2 * T]), reads=[b_rstd[1]], writes=[b_rstd[1]])
            for c in range(KC):
                k = 2 + c % 2
                P.op("dve", lambda e, c=c, k=k: e.scalar_tensor_tensor(
                    out=tmpf[:, k * T:(k + 1) * T], in0=ybuf[:, c * T:(c + 1) * T], scalar=coef[:, gofs + c:gofs + c + 1],
                    in1=rstd[:, T:2 * T], op0=ALU.mult, op1=ALU.mult), reads=[b_y[c], b_rstd[1], b_coef], writes=[b_tmp[k]])
                P.op("dve", lambda e, c=c, k=k: e.tensor_tensor(out=xc(c), in0=xc(c), in1=tmpf[:, k * T:(k + 1) * T], op=ALU.add),
                     reads=[b_tmp[k], b_x[c]], writes=[b_x[c]])

        def out_proj_to_y(j, b):
            k = j % 2
            P.op("act", lambda e, j=j, b=b: e.activation(out=ybuf[:, j * T:(j + 1) * T], in_=banks[b][:], func=AF.Identity),
                 reads=[bankb[b]], writes=[b_y[j]])
            P.op("act", lambda e, b=b, k=k: e.activation(out=sq[:, k * T:(k + 1) * T], in_=banks[b][:], func=AF.Square),
                 reads=[bankb[b]], writes=[b_sq[k]])
            pending_stat.append((j, k))

        pending_stat = []

        def flush_stat(all_=False):
            while pending_stat and (all_ or len(pending_stat) > 1):
                j, k = pending_stat.pop(0)
                P.op("pe", lambda e, j=j, k=k: e.matmul(banks[STAT][:], onesD[:], sq[:, k * T:(k + 1) * T],
                                                          start=(j == 0), stop=(j == KC - 1)),
                     reads=[b_sq[k], b_const], writes=[bankb[STAT]])

        def dbg(name, ap, bufs, eng="sp"):
            if debug:
                P.dma(eng, lambda e: e.dma_start(out=dbg_out[name], in_=ap), sem_o, reads=bufs)

        for step in range(n_steps):
            for qd in range(4):
                P.dma("sp", lambda e, qd=qd, step=step: e.dma_start(
                    out=xres[:, qd * 2048:(qd + 1) * 2048], in_=xin[step][:, qd * 2048:(qd + 1) * 2048]),
                    sem_x, writes=b_x[qd * 4:(qd + 1) * 4])

            if fused and step >= 1:
                for hv in range(2):
                    P.dma("sp", lambda e, hv=hv: e.dma_start(out=ybuf[:, hv * 4096:(hv + 1) * 4096], in_=gath[hv][0:128, :]),
                          sem_r, reads=[b_gath[hv]], writes=b_y[hv * 8:(hv + 1) * 8])
                for c in range(KC):
                    P.op("dve", lambda e, c=c: e.scalar_tensor_tensor(
                        out=xc(c), in0=ybuf[:, c * T:(c + 1) * T], scalar=flagt[:, 0:1], in1=xc(c), op0=ALU.mult, op1=ALU.add),
                        reads=[b_y[c], b_x[c], b_const], writes=[b_x[c]])

            norm_in(0, b_coefp[0])
            dbg("d_h", hbuf[:].bitcast(F32), b_h)
            if step == 0:
                dbg("d_mod", modv[:], b_modp)

            for kc in range(KC):
                P.op("pe", lambda e, kc=kc: e.matmul(banks[MISC][0:16, :], wlrt[:, kc * 16:(kc + 1) * 16], hc(kc),
                                                      start=(kc == 0), stop=(kc == KC - 1)),
                     reads=[b_h[kc], b_const], writes=[bankb[MISC]])
            P.op("act", lambda e: e.activation(out=lrT[0:16, :], in_=banks[MISC][0:16, :], func=AF.Identity),
                 reads=[bankb[MISC]], writes=[b_lrT])

            def mm_tok(piece):
                s = load_piece(win[piece], 2048)
                b = next_bank()
                fns = []
                for tb in range(4):
                    for kc in range(KC):
                        fns.append(lambda e, s=s, b=b, tb=tb, kc=kc: e.matmul(
                            banks[b][:, tb * 128:(tb + 1) * 128], hbuf[:, kc * T + tb * 128: kc * T + (tb + 1) * 128],
                            slot(s)[:, kc * 128:(kc + 1) * 128], start=(kc == 0), stop=(kc == KC - 1)))
                P.op("pe", fns, reads=[b_slot[s]] + b_h, writes=[bankb[b]])
                return b

            kd3 = kdec.rearrange("p (tb c) -> p tb c", tb=4)
            E3 = Ef.rearrange("p (tb c) -> p tb c", tb=4)
            v3 = vtm.rearrange("p (tb c) -> p tb c", tb=4)

            def v_piece(jv):
                b = mm_tok(PV + jv)
                P.op("act", lambda e, b=b, jv=jv: e.activation(
                    out=v3[:, :, jv * 128:(jv + 1) * 128], in_=banks[b][:].rearrange("p (tb c) -> p tb c", tb=4),
                    func=AF.Identity), reads=[bankb[b]], writes=[b_v])

            for jv in range(4):
                v_piece(jv)
            for tb in range(4):
                b = next_bank()
                P.op("pe", lambda e, tb=tb, b=b: e.matmul(banks[b][:], lrT[0:17, tb * 128:(tb + 1) * 128], wa2t[0:17, :],
                                                           start=True, stop=True),
                     reads=[b_lrT, b_const], writes=[bankb[b]])
                k = tb % 2
                P.op("act", lambda e, b=b, k=k: e.activation(out=tmpf[:, k * T:(k + 1) * T], in_=banks[b][:], func=AF.Exp, scale=-1.0),
                     reads=[bankb[b]], writes=[b_tmp[k]])
                P.op("act", lambda e, tb=tb, k=k: e.activation(out=laf[:, tb * T:(tb + 1) * T], in_=tmpf[:, k * T:(k + 1) * T],
                                                                 func=AF.Ln, bias=1.0, scale=1.0),
                     reads=[b_tmp[k]], writes=[b_la[tb]])
            dbg("d_la", laf, b_la)
            for jv in range(4, 8):
                v_piece(jv)
            for tb in range(4):
                b = next_bank()
                P.op("pe", lambda e, tb=tb, b=b: e.matmul(banks[b][:], maskU[:], laf[:, tb * T:(tb + 1) * T], start=True, stop=True),
                     reads=[b_la[tb], b_const], writes=[bankb[b]])
                P.op("act", lambda e, tb=tb, b=b: e.activation(out=Ef[:, tb * T:(tb + 1) * T], in_=banks[b][:], func=AF.Exp, scale=-1.0 / 16),
                     reads=[bankb[b]], writes=[b_E])
            dbg("d_E", Ef, [b_E])
            fns = []
            for tb in range(4):
                for hd in range(4):
                    col = (tb * 4 + hd) * 2
                    fns.append(lambda e, tb=tb, hd=hd, col=col: e.matmul(
                        banks[MISC][:, col:col + 2], laf[:, tb * T + hd * 128: tb * T + (hd + 1) * 128], cindt[:],
                        start=True, stop=True))
            P.op("pe", fns, reads=b_la + [b_const], writes=[bankb[MISC]])
            P.op("act", lambda e: e.activation(out=decay[:], in_=banks[MISC][:, 0:32], func=AF.Exp, scale=-1.0 / 16),
                 reads=[bankb[MISC]], writes=[b_decay])
            dbg("d_dec", decay[:], [b_decay])
            for hd in range(4):
                b = mm_std(win[PQ + hd], hc, KC, b_h)
                P.op("act", lambda e, b=b, hd=hd: e.activation(out=qT[:, hd * T:(hd + 1) * T], in_=banks[b][:], func=AF.Identity,
                                                                 scale=float(128 ** -0.5)), reads=[bankb[b]], writes=[b_q[hd]])
            for hd in range(4):
                b = mm_tok(PK + hd)
                P.op("dve", lambda e, b=b, hd=hd: e.tensor_tensor(
                    out=kd3[:, :, hd * 128:(hd + 1) * 128], in0=banks[b][:].rearrange("p (tb c) -> p tb c", tb=4),
                    in1=E3[:, :, hd * 128:(hd + 1) * 128], op=ALU.mult), reads=[bankb[b], b_E], writes=[b_kdec])
            dbg("d_q", qT.bitcast(F32), b_q)
            dbg("d_k", kdec.bitcast(F32), [b_kdec])
            dbg("d_v", vtm.bitcast(F32), [b_v])
            if step == 0:
                mod_part(1)

            o3 = oT.rearrange("p (c t) -> p c t", c=8)
            UB = [6, 6, 7, 7]
            OB = MISC

            def gla_out(ci):
                kk = ci % 2
                fns = []
                for hd in range(4):
                    for hf in range(2):
                        col = (hd * 2 + hf) * 64
                        fns.append(lambda e, hd=hd, kk=kk, hf=hf, ci=ci, col=col: e.matmul(
                            banks[OB][:, col:col + 64],
                            Sb[:, kk * 1024 + hd * 256 + hf * 128: kk * 1024 + hd * 256 + (hf + 1) * 128],
                            qT[:, hd * T + ci * 64: hd * T + (ci + 1) * 64], start=True, stop=True))
                P.op("pe", fns, reads=[b_Sb[kk]] + b_q, writes=[bankb[OB]])
                P.op("act", lambda e, ci=ci: e.activation(
                    out=o3[:, :, ci * 64:(ci + 1) * 64], in_=banks[OB][:].rearrange("p (c t) -> p c t", c=8), func=AF.Identity),
                    reads=[bankb[OB]], writes=b_o)

            def r_piece(jr):
                b = mm_std(win[PR + jr], hc, KC, b_h)
                P.op("act", lambda e, b=b, jr=jr: e.activation(out=rs[:, jr * T:(jr + 1) * T], in_=banks[b][:], func=AF.Silu),
                     reads=[bankb[b]], writes=[b_rs[jr]])

            for ci in range(8):
                tb, half = ci // 2, ci % 2
                r0 = half * 64
                kk = ci % 2
                for hd in range(4):
                    ub = UB[hd]
                    uc = (hd % 2) * 256
                    P.op("pe", lambda e, hd=hd, tb=tb, r0=r0, ub=ub, uc=uc: e.matmul(
                        banks[ub][:, uc:uc + 256], kdec[r0:r0 + 64, tb * 512 + hd * 128: tb * 512 + (hd + 1) * 128],
                        vtm[r0:r0 + 64, tb * 1024 + hd * 256: tb * 1024 + (hd + 1) * 256], start=True, stop=True),
                        reads=[b_kdec, b_v], writes=[bankb[ub]])
                for hd in range(4):
                    ub = UB[hd]
                    uc = (hd % 2) * 256
                    dcol = (tb * 4 + hd) * 2 + half
                    P.op("dve", lambda e, hd=hd, dcol=dcol, ub=ub, uc=uc: e.scalar_tensor_tensor(
                        out=S32[:, hd * 256:(hd + 1) * 256], in0=S32[:, hd * 256:(hd + 1) * 256], scalar=decay[:, dcol:dcol + 1],
                        in1=banks[ub][:, uc:uc + 256], op0=ALU.mult, op1=ALU.add), reads=[bankb[ub], b_decay, b_S[hd]], writes=[b_S[hd]])
                P.op("act", lambda e, kk=kk: e.activation(out=Sb[:, kk * 1024:(kk + 1) * 1024], in_=S32[:], func=AF.Identity),
                     reads=b_S, writes=[b_Sb[kk]])
                r_piece(ci)
                if ci >= 1:
                    gla_out(ci - 1)
            gla_out(7)
            dbg("d_o", oT, b_o)

            for hd in range(4):
                for hf in range(2):
                    c = hd * 2 + hf
                    k = hf
                    P.op("act", lambda e, c=c, k=k: e.activation(out=sq[:, k * T:(k + 1) * T], in_=oT[:, c * T:(c + 1) * T], func=AF.Square),
                         reads=[b_o[c]], writes=[b_sq[k]])
                    P.op("pe", lambda e, k=k, hf=hf: e.matmul(banks[STAT][:], ones256[:], sq[:, k * T:(k + 1) * T],
                                                                start=(hf == 0), stop=(hf == 1)),
                         reads=[b_sq[k], b_const], writes=[bankb[STAT]])
                P.op("act", lambda e: e.activation(out=rstd[:, T:2 * T], in_=banks[STAT][:], func=AF.Sqrt, bias=epst[:, 0:1], scale=1.0),
                     reads=[bankb[STAT], b_const], writes=[b_rstd[1]])
                P.op("dve", lambda e: e.reciprocal(out=rstd[:, T:2 * T], in_=rstd[:, T:2 * T]), reads=[b_rstd[1]], writes=[b_rstd[1]])
                for hf in range(2):
                    c = hd * 2 + hf
                    k = 2 + hf
                    P.op("dve", lambda e, c=c, k=k, hf=hf: e.scalar_tensor_tensor(
                        out=tmpf[:, k * T:(k + 1) * T], in0=oT[:, c * T:(c + 1) * T], scalar=glagt[:, hf:hf + 1],
                        in1=rstd[:, T:2 * T], op0=ALU.mult, op1=ALU.mult), reads=[b_o[c], b_rstd[1], b_const], writes=[b_tmp[k]])
                    P.op("dve", lambda e, c=c, k=k: e.tensor_tensor(
                        out=gin[:, c * T:(c + 1) * T], in0=tmpf[:, k * T:(k + 1) * T], in1=rs[:, c * T:(c + 1) * T], op=ALU.mult),
                        reads=[b_tmp[k], b_rs[c]], writes=[b_gin[c]])
            dbg("d_gin", gin.bitcast(F32), b_gin)

            if step == 0:
                mod_part(2)
            for c in range(8):
                k = c % 2
                ub = ubuf[:, k * (T + 2):(k + 1) * (T + 2)]
                cb_ = cbuf[:, k * T:(k + 1) * T]
                b1 = mm_std(win[PCC + c], hc, KC, b_h)
                P.op("act", lambda e, b1=b1, k=k: e.activation(out=tmpf[:, k * T:(k + 1) * T], in_=banks[b1][:], func=AF.Identity),
                     reads=[bankb[b1]], writes=[b_tmp[k]])
                b2 = mm_std(win[PCX + c], hc, KC, b_h)
                P.op("dve", lambda e, ub=ub, c=c: e.tensor_copy(out=ub[:, 0:2], in_=uhalo[:, 2 * c:2 * c + 2]),
                     reads=[b_uh[c]], writes=[b_ubuf[k]])
                P.op("dve", lambda e, ub=ub, b2=b2, k=k: e.tensor_tensor(out=ub[:, 2:T + 2], in0=banks[b2][:], in1=tmpf[:, k * T:(k + 1) * T], op=ALU.mult),
                     reads=[bankb[b2], b_tmp[k], b_ubuf[k]], writes=[b_ubuf[k]])
                P.op("dve", lambda e, ub=ub, c=c: e.tensor_copy(out=uhalo[:, 2 * c:2 * c + 2], in_=ub[:, T:T + 2]),
                     reads=[b_ubuf[k]], writes=[b_uh[c]])
                P.op("dve", lambda e, ub=ub, cb_=cb_, c=c: e.tensor_scalar(out=cb_, in0=ub[:, 0:T], scalar1=cmwt[:, 3 * c:3 * c + 1], scalar2=None, op0=ALU.mult),
                     reads=[b_ubuf[k], b_const], writes=[b_cbuf[k]])
                P.op("dve", lambda e, ub=ub, cb_=cb_, c=c: e.scalar_tensor_tensor(out=cb_, in0=ub[:, 1:T + 1], scalar=cmwt[:, 3 * c + 1:3 * c + 2], in1=cb_, op0=ALU.mult, op1=ALU.add),
                     reads=[b_ubuf[k], b_cbuf[k]], writes=[b_cbuf[k]])
                P.op("dve", lambda e, ub=ub, cb_=cb_, c=c: e.scalar_tensor_tensor(out=cb_, in0=ub[:, 2:T + 2], scalar=cmwt[:, 3 * c + 2:3 * c + 3], in1=cb_, op0=ALU.mult, op1=ALU.add),
                     reads=[b_ubuf[k], b_cbuf[k]], writes=[b_cbuf[k]])
                b3 = mm_std(win[PCB + c], hc, KC, b_h)
                P.op("dve", lambda e, b3=b3, cb_=cb_, c=c: e.tensor_tensor(out=cvb[:, c * T:(c + 1) * T], in0=banks[b3][:], in1=cb_, op=ALU.mult),
                     reads=[bankb[b3], b_cbuf[k]], writes=[b_cvb[c]])
            dbg("d_cvb", cvb.bitcast(F32), b_cvb)

            for j in range(16):
                ka, kb = (j % 2) * 2, (j % 2) * 2 + 1
                ta = tmpf[:, ka * T:(ka + 1) * T]
                tbv = tmpf[:, kb * T:(kb + 1) * T]
                b1 = mm_std(win[PGA + j], hc, KC, b_h)
                P.op("act", lambda e, b1=b1, ta=ta: e.activation(out=ta, in_=banks[b1][:], func=AF.Sigmoid),
                     reads=[bankb[b1]], writes=[b_tmp[ka]])
                b2 = mm_std(woutg[j], lambda kc: gin[:, kc * T:(kc + 1) * T], 8, b_gin)
                P.op("dve", lambda e, b2=b2, ta=ta: e.tensor_tensor(out=ta, in0=banks[b2][:], in1=ta, op=ALU.mult),
                     reads=[bankb[b2], b_tmp[ka]], writes=[b_tmp[ka]])
                b3 = mm_std(win[PGB + j], hc, KC, b_h)
                P.op("act", lambda e, b3=b3, tbv=tbv: e.activation(out=tbv, in_=banks[b3][:], func=AF.Sigmoid),
                     reads=[bankb[b3]], writes=[b_tmp[kb]])
                b4 = mm_std(woutc[j], lambda kc: cvb[:, kc * T:(kc + 1) * T], 8, b_cvb)
                P.op("dve", lambda e, b4=b4, tbv=tbv: e.tensor_tensor(out=tbv, in0=banks[b4][:], in1=tbv, op=ALU.mult),
                     reads=[bankb[b4], b_tmp[kb]], writes=[b_tmp[kb]])
                P.op("dve", lambda e, j=j, ta=ta, tbv=tbv: e.tensor_tensor(out=mbuf[:, j * T:(j + 1) * T], in0=ta, in1=tbv, op=ALU.add),
                     reads=[b_tmp[ka], b_tmp[kb]], writes=[b_m[j]])
            dbg("d_m", mbuf.bitcast(F32), b_m)

            for j in range(16):
                b = mm_std(wo[j], lambda kc: mbuf[:, kc * T:(kc + 1) * T], KC, b_m)
                flush_stat()
                out_proj_to_y(j, b)
            flush_stat(True)
            residual(32, b_coefp[1])
            dbg("d_xmid", xres[:], b_x)

            norm_in(48, b_coefp[2])
            for j in range(FC):
                k = j % 2
                ub = ubuf[:, k * (T + 2):(k + 1) * (T + 2)]
                cb_ = cbuf[:, k * T:(k + 1) * T]
                b1 = mm_std(wup[j], hc, KC, b_h, per_kc=(j == 0))
                if step == 0 and j == 4:
                    mod_part(3)
                P.op("dve", lambda e, ub=ub, j=j: e.tensor_copy(out=ub[:, 0:2], in_=ghalo[:, 2 * j:2 * j + 2]),
                     reads=[b_gh[j]], writes=[b_ubuf[k]])
                P.op("act", lambda e, ub=ub, b1=b1: e.activation(out=ub[:, 2:T + 2], in_=banks[b1][:], func=AF.Identity),
                     reads=[bankb[b1], b_ubuf[k]], writes=[b_ubuf[k]])
                P.op("dve", lambda e, ub=ub, j=j: e.tensor_copy(out=ghalo[:, 2 * j:2 * j + 2], in_=ub[:, T:T + 2]),
                     reads=[b_ubuf[k]], writes=[b_gh[j]])
                P.op("dve", lambda e, ub=ub, cb_=cb_, j=j: e.tensor_scalar(out=cb_, in0=ub[:, 0:T], scalar1=fcwt[:, 3 * j:3 * j + 1], scalar2=None, op0=ALU.mult),
                     reads=[b_ubuf[k], b_const], writes=[b_cbuf[k]])
                P.op("dve", lambda e, ub=ub, cb_=cb_, j=j: e.scalar_tensor_tensor(out=cb_, in0=ub[:, 1:T + 1], scalar=fcwt[:, 3 * j + 1:3 * j + 2], in1=cb_, op0=ALU.mult, op1=ALU.add),
                     reads=[b_ubuf[k], b_cbuf[k]], writes=[b_cbuf[k]])
                P.op("dve", lambda e, ub=ub, cb_=cb_, j=j: e.scalar_tensor_tensor(out=cb_, in0=ub[:, 2:T + 2], scalar=fcwt[:, 3 * j + 2:3 * j + 3], in1=cb_, op0=ALU.mult, op1=ALU.add),
                     reads=[b_ubuf[k], b_cbuf[k]], writes=[b_cbuf[k]])
                P.op("act", lambda e, cb_=cb_, k=k: e.activation(out=tmpf[:, k * T:(k + 1) * T], in_=cb_, func=AF.Gelu),
                     reads=[b_cbuf[k]], writes=[b_tmp[k]])
                b2 = mm_std(wup[FC + j], hc, KC, b_h)
                P.op("dve", lambda e, b2=b2, j=j, k=k: e.tensor_tensor(out=actb[:, j * T:(j + 1) * T], in0=banks[b2][:], in1=tmpf[:, k * T:(k + 1) * T], op=ALU.mult),
                     reads=[bankb[b2], b_tmp[k]], writes=[b_act[j]])
            for j in range(16):
                b = next_bank()
                for g in range(4):
                    s = load_piece(wdn[j * 4 + g], 1408)
                    fns = []
                    for kk in range(11):
                        kc = g * 11 + kk
                        fns.append(lambda e, s=s, kk=kk, kc=kc, b=b: e.matmul(
                            banks[b][:], slot(s)[:, kk * 128:(kk + 1) * 128], actb[:, kc * T:(kc + 1) * T],
                            start=(kc == 0), stop=(kc == FC - 1)))
                    P.op("pe", fns, reads=[b_slot[s]] + b_act[g * 11:(g + 1) * 11], writes=[bankb[b]])
                flush_stat()
                out_proj_to_y(j, b)
            flush_stat(True)
            residual(80, b_coefp[3])

            if fused and step < n_steps - 1:
                for hv in range(2):
                    P.dma("sp", lambda e, hv=hv: e.dma_start(out=outbox[hv][:, :], in_=xres[:, hv * 4096:(hv + 1) * 4096]),
                          sem_b, reads=b_x[hv * 8:(hv + 1) * 8], writes=[b_outbox[hv]])
                prev = None
                for hv in range(2):
                    prev = P.dma("pool", lambda e, hv=hv: e.collective_compute(
                        "AllGather", ALU.bypass, replica_groups=[[0, 1], [2, 3], [4, 5], [6, 7]],
                        ins=[outbox[hv].ap().opt()], outs=[gath[hv].ap().opt()]),
                        sem_cc, reads=[b_outbox[hv]], writes=[b_gath[hv]], inc=1)
            for qd in range(4):
                P.dma("sp", lambda e, qd=qd, step=step: e.dma_start(
                    out=xout[step][:, qd * 2048:(qd + 1) * 2048], in_=xres[:, qd * 2048:(qd + 1) * 2048]),
                    sem_o, reads=b_x[qd * 4:(qd + 1) * 4])
            if fused and step == 0:
                P.op("dve", lambda e: e.tensor_scalar(out=S32[:], in0=S32[:], scalar1=flagt[:, 1:2], scalar2=None, op0=ALU.mult),
                     reads=b_S + [b_const], writes=b_S)
                P.op("dve", lambda e: e.tensor_scalar(out=uhalo[:], in0=uhalo[:], scalar1=flagt[:, 1:2], scalar2=None, op0=ALU.mult),
                     reads=b_uh + [b_const], writes=b_uh)
                P.op("dve", lambda e: e.tensor_scalar(out=ghalo[:], in0=ghalo[:], scalar1=flagt[:, 1:2], scalar2=None, op0=ALU.mult),
                     reads=b_gh + [b_const], writes=b_gh)

        P.dma("sp", lambda e: e.dma_start(out=s_out, in_=S32[:]), sem_o, reads=b_S)
        P.dma("sp", lambda e: e.dma_start(out=uh_out, in_=uhalo[:]), sem_o, reads=b_uh)
        last = P.dma("sp", lambda e: e.dma_start(out=gh_out, in_=ghalo[:]), sem_o, reads=b_gh)

        print("sbuf bytes remaining", nc.sbuf_bytes_remaining)
        P.finalize(engsem)
        final_o = P.dma_counts[id(sem_o)]

        block = E(nc.Block())

        @block.tensor
        def _(e):
            P.replay("pe", e)

        @block.scalar
        def _(e):
            P.replay("act", e)

        @block.vector
        def _(e):
            P.replay("dve", e)

        @block.gpsimd
        def _(e):
            P.replay("pool", e)

        @block.sync
        def _(e):
            P.replay("sp", e)
            e.wait_ge(sem_o, final_o)

    return nc


def _pieces(w, col0s, nk):
    out = np.empty((len(col0s), 128, nk * 128), np.float32)
    wk = w.reshape(nk, 128, w.shape[1])
    for j, c0 in enumerate(col0s):
        out[j] = wk[:, :, c0:c0 + 128].transpose(1, 0, 2).reshape(128, nk * 128)
    return out


def _fm(v):
    return np.ascontiguousarray(v.reshape(-1, 128).T)


_MASKU = None


def _consts():
    global _MASKU
    if _MASKU is None:
        j = np.arange(128)[:, None]
        l = np.arange(128)[None, :]
        mu = ((j > l) & ((j // 64) == (l // 64))).astype(np.float32)
        ci = np.zeros((128, 2), np.float32)
        ci[:64, 0] = 1.0
        ci[64:, 1] = 1.0
        _MASKU = (mu, ci)
    return _MASKU


def layer_maps(l, w_ada, b_ada, norm_g, w_in, w_a2, b_a2, gla_norm_g, w_out_gla, conv_mix_w,
               w_out_conv, w_o, w_up, ffn_conv_w, w_down):
    col0 = [j * 128 for j in range(24)] + [3088 + j * 128 for j in range(56)]
    mu, ci = _consts()
    m = {}
    m["wada"] = _pieces(w_ada[l], [j * 128 for j in range(96)], 16)
    m["bada"] = _fm(b_ada[l])
    m["ng"] = np.ascontiguousarray(np.concatenate([_fm(norm_g[l, i]) for i in range(4)], axis=1))
    m["win"] = _pieces(w_in[l], col0, 16)
    m["wlr"] = np.ascontiguousarray(w_in[l][:, 3072:3088].reshape(16, 128, 16).transpose(1, 0, 2).reshape(128, 256))
    m["wa2e"] = np.ascontiguousarray(np.concatenate([w_a2[l], b_a2[l][None, :]], axis=0))
    m["glag"] = _fm(gla_norm_g[l])
    m["woutg"] = _pieces(w_out_gla[l], [j * 128 for j in range(16)], 8)
    m["woutc"] = _pieces(w_out_conv[l], [j * 128 for j in range(16)], 8)
    m["cmw"] = np.ascontiguousarray(conv_mix_w[l].reshape(3, 8, 128).transpose(2, 1, 0).reshape(128, 24))
    m["wo"] = _pieces(w_o[l], [j * 128 for j in range(16)], 16)
    m["wup"] = _pieces(w_up[l], [j * 128 for j in range(88)], 16)
    m["fcw"] = np.ascontiguousarray(ffn_conv_w[l].reshape(3, FC, 128).transpose(2, 1, 0).reshape(128, 132))
    wd = _pieces(w_down[l], [j * 128 for j in range(16)], FC)
    m["wdn"] = np.ascontiguousarray(wd.reshape(16, 128, 4, 1408).transpose(0, 2, 1, 3).reshape(64, 128, 1408))
    m["masku"] = mu
    m["cind"] = ci
    return m


def x_tiles(xb):
    s = xb.shape[0]
    return np.ascontiguousarray(xb.reshape(s // T, T, KC, 128).transpose(0, 3, 2, 1).reshape(s // T, 128, KC * T))


def x_untile(t):
    n = t.shape[0]
    return np.ascontiguousarray(t.reshape(n, 128, KC, T).transpose(0, 3, 2, 1).reshape(n * T, D))


_NC_CACHE = {}


def _get_nc():
    if "nc" not in _NC_CACHE:
        _NC_CACHE["nc"] = build(NTILE + 1, fused=True)
    return _NC_CACHE["nc"]


def kernel(x, c, w_ada, b_ada, norm_g, w_in, w_a2, b_a2, gla_norm_g, w_out_gla,
           conv_mix_w, w_out_conv, w_o, w_up, ffn_conv_w, w_down):
    args = [np.asarray(a, np.float32) for a in (w_ada, b_ada, norm_g, w_in, w_a2, b_a2, gla_norm_g, w_out_gla,
                                                conv_mix_w, w_out_conv, w_o, w_up, ffn_conv_w, w_down)]
    x = np.asarray(x, np.float32)
    c = np.asarray(c, np.float32)
    B = x.shape[0]
    nc = _get_nc()
    lm = [layer_maps(l, *args) for l in range(2)]
    zeros_x = np.zeros((NTILE + 1, 128, KC * T), np.float32)
    fl = [np.tile(np.array([[0.0, 1.0]], np.float32), (128, 1)), np.tile(np.array([[1.0, 0.0]], np.float32), (128, 1))]
    in_maps = []
    for core in range(2 * B):
        b, l = core // 2, core % 2
        m = dict(lm[l])
        m["cvec"] = _fm(c[b])
        if l == 0:
            m["xin"] = np.concatenate([x_tiles(x[b]), zeros_x[:1]], axis=0)
        else:
            m["xin"] = zeros_x
        m["flags"] = fl[l]
        m["s_in"] = np.zeros((128, 1024), np.float32)
        m["uh_in"] = np.zeros((128, 16), np.float32)
        m["gh_in"] = np.zeros((128, 88), np.float32)
        in_maps.append(m)
    res = run_bass_kernel_spmd(nc, in_maps, core_ids=list(range(2 * B)))
    out = np.stack([x_untile(np.asarray(res.results[2 * b + 1]["xout"])[1:]) for b in range(B)], axis=0)
    return out.astype(np.float32)
```

```python
import numpy as np
from contextlib import ExitStack
import concourse.bass as bass
import concourse.mybir as mybir
from concourse.bass_utils import run_bass_kernel_spmd

F32 = mybir.dt.float32
BF16 = mybir.dt.bfloat16
AF = mybir.ActivationFunctionType
ALU = mybir.AluOpType

D = 2048
KC = 16
T = 512
NTILE = 8
DFF = 5632
FC = 44
EPS = 1e-6
NSLOT = 6
SLOTW = 2048

PQ, PK, PV, PR, PCB, PCC, PCX, PGA, PGB = 0, 4, 8, 16, 24, 32, 40, 48, 64


class Op:
    __slots__ = ("eng", "fns", "deps", "signal", "val", "dma_sem", "dma_val", "inc")

    def __init__(self, eng, fns, deps):
        self.eng = eng
        self.fns = fns
        self.deps = deps
        self.signal = False
        self.val = None
        self.dma_sem = None
        self.dma_val = None
        self.inc = 16


class Buf:
    __slots__ = ("w", "r", "lo", "hi", "psum", "last")

    def __init__(self, lo=None, hi=None, psum=False):
        self.w = None
        self.r = []
        self.lo = lo
        self.hi = hi
        self.psum = psum
        self.last = None


class Prog:
    def __init__(self):
        self.ops = {"pe": [], "act": [], "dve": [], "pool": [], "sp": []}
        self.arena_bufs = []
        self.dma_counts = {}

    def abuf(self, lo, hi):
        b = Buf(lo, hi)
        self.arena_bufs.append(b)
        return b

    def _deps(self, eng, reads, writes):
        deps = []
        for b in reads:
            if b.psum:
                if b.last is not None and b.last.eng != eng:
                    deps.append(b.last)
            elif b.w is not None:
                deps.append(b.w)
        for b in writes:
            if b.psum:
                if b.last is not None and b.last.eng != eng:
                    deps.append(b.last)
                continue
            if b.w is not None:
                deps.append(b.w)
            deps.extend(b.r)
            if b.lo is not None:
                for o in self.arena_bufs:
                    if o is not b and o.lo < b.hi and b.lo < o.hi:
                        if o.w is not None:
                            deps.append(o.w)
                        deps.extend(o.r)
        return deps

    def _note(self, op, reads, writes):
        for b in reads:
            if b.psum:
                b.last = op
            else:
                b.r.append(op)
        for b in writes:
            if b.psum:
                b.last = op
                continue
            b.w = op
            b.r = []
            if b.lo is not None:
                for o in self.arena_bufs:
                    if o is not b and o.lo < b.hi and b.lo < o.hi:
                        o.w = None
                        o.r = []

    def op(self, eng, fns, reads=(), writes=(), deps=()):
        if not isinstance(fns, (list, tuple)):
            fns = [fns]
        d = self._deps(eng, reads, writes) + [x for x in deps if x is not None]
        o = Op(eng, list(fns), d)
        self.ops[eng].append(o)
        self._note(o, reads, writes)
        return o

    def dma(self, eng, fn, sem, reads=(), writes=(), deps=(), inc=16):
        o = self.op(eng, fn, reads, writes, deps)
        c = self.dma_counts.get(id(sem), 0) + inc
        self.dma_counts[id(sem)] = c
        o.dma_sem = sem
        o.dma_val = c
        o.inc = inc
        return o

    def finalize(self, engsem):
        for lst in self.ops.values():
            for o in lst:
                for dd in o.deps:
                    if dd.dma_sem is None:
                        dd.signal = True
        for eng, lst in self.ops.items():
            c = 0
            for o in lst:
                if o.signal:
                    c += 1
                    o.val = c
        self.engsem = engsem

    def replay(self, eng, e):
        seen = {}
        for o in self.ops[eng]:
            for dd in o.deps:
                if dd.dma_sem is not None:
                    sem, val = dd.dma_sem, dd.dma_val
                else:
                    if dd.eng == eng and False:
                        continue
                    sem, val = self.engsem[dd.eng], dd.val
                k = id(sem)
                if seen.get(k, 0) >= val:
                    continue
                seen[k] = val
                e.wait_ge(sem, val)
            ins = None
            for f in o.fns:
                ins = f(e)
            if o.dma_sem is not None:
                if o.inc == 16:
                    ins.then_inc(o.dma_sem, 16)
                else:
                    ins.then_inc(o.dma_sem)
            elif o.signal:
                ins.then_inc(self.engsem[eng], 1)


def build(n_steps=1, debug=False, fused=False):
    nc = bass.Bass("TRN2", target_bir_lowering=False)

    def din(name, shape):
        return nc.dram_tensor(name, list(shape), F32, kind="ExternalInput").ap()

    def dout(name, shape):
        return nc.dram_tensor(name, list(shape), F32, kind="ExternalOutput").ap()

    xin = din("xin", [n_steps, 128, KC * T])
    cvec = din("cvec", [128, KC])
    wada = din("wada", [96, 128, 2048])
    bada = din("bada", [128, 96])
    ngd = din("ng", [128, 64])
    win = din("win", [80, 128, 2048])
    wlr = din("wlr", [128, 256])
    wa2e = din("wa2e", [17, 512])
    glag = din("glag", [128, 2])
    woutg = din("woutg", [16, 128, 1024])
    woutc = din("woutc", [16, 128, 1024])
    cmw = din("cmw", [128, 24])
    wo = din("wo", [16, 128, 2048])
    wup = din("wup", [88, 128, 2048])
    fcw = din("fcw", [128, 132])
    wdn = din("wdn", [64, 128, 1408])
    masku = din("masku", [128, 128])
    cind = din("cind", [128, 2])
    s_in = din("s_in", [128, 1024])
    uh_in = din("uh_in", [128, 16])
    gh_in = din("gh_in", [128, 88])
    flags = din("flags", [128, 2])
    if fused:
        outbox = [nc.dram_tensor(f"outbox{i}", [128, 4096], F32) for i in range(2)]
        gath = [nc.dram_tensor(f"gath{i}", [256, 4096], F32) for i in range(2)]
    xout = dout("xout", [n_steps, 128, KC * T])
    s_out = dout("s_out", [128, 1024])
    uh_out = dout("uh_out", [128, 16])
    gh_out = dout("gh_out", [128, 88])
    dbg_out = {}
    if debug:
        for nm, w in [("d_h", KC * T // 2), ("d_la", 2048), ("d_E", 2048), ("d_dec", 32), ("d_q", 1024),
                      ("d_k", 1024), ("d_v", 2048), ("d_o", 4096), ("d_gin", 2048), ("d_cvb", 2048),
                      ("d_m", 4096), ("d_xmid", KC * T), ("d_mod", 96)]:
            dbg_out[nm] = dout(nm, [128, w])

    P = Prog()
    with ExitStack() as es:
        E = es.enter_context

        def sb(name, shape, dt):
            return E(nc.sbuf_tensor(name, list(shape), dt))

        xres = sb("xres", [128, KC * T], F32)
        hbuf = sb("hbuf", [128, KC * T], BF16)
        arena = sb("arena", [128, 40960], BF16)
        ring = sb("ring", [128, NSLOT * SLOTW], BF16)
        sq = sb("sq", [128, 2 * T], BF16)
        rstd = sb("rstd", [128, 2 * T], F32)
        tmpf = sb("tmpf", [128, 4 * T], F32)
        ubuf = sb("ubuf", [128, 2 * (T + 2)], F32)
        cbuf = sb("cbuf", [128, 2 * T], F32)
        S32 = sb("S32", [128, 1024], F32)
        Sb = sb("Sb", [128, 2048], BF16)
        lrT = sb("lrT", [32, T], BF16)
        onesD = sb("onesD", [128, 128], BF16)
        ones256 = sb("ones256", [128, 128], BF16)
        maskU = sb("maskU", [128, 128], F32)
        cindt = sb("cindt", [128, 2], F32)
        wa2t = sb("wa2t", [32, 512], BF16)
        wlrt = sb("wlrt", [128, 256], BF16)
        modv = sb("modv", [128, 96], F32)
        badat = sb("badat", [128, 96], F32)
        ngt = sb("ngt", [128, 64], F32)
        coef = sb("coef", [128, 96], F32)
        glagt = sb("glagt", [128, 2], F32)
        cmwt = sb("cmwt", [128, 24], F32)
        fcwt = sb("fcwt", [128, 132], F32)
        cst = sb("cst", [128, KC], F32)
        csb = sb("csb", [128, KC], BF16)
        decay = sb("decay", [128, 32], F32)
        uhalo = sb("uhalo", [128, 16], F32)
        ghalo = sb("ghalo", [128, 88], F32)
        epst = sb("epst", [128, 1], F32)
        flagt = sb("flagt", [128, 2], F32)

        banks = [E(nc.psum_tensor(f"bank{i}", [128, 512], F32)) for i in range(8)]
        bankb = [Buf(psum=True) for _ in range(8)]

        engsem = {k: E(nc.semaphore(f"sem_{k}")) for k in ["pe", "act", "dve", "pool", "sp"]}
        slot_sem = [E(nc.semaphore(f"slot{i}")) for i in range(NSLOT)]
        sem_x = E(nc.semaphore("sem_x"))
        sem_c = E(nc.semaphore("sem_c"))
        sem_o = E(nc.semaphore("sem_o"))
        sem_b = E(nc.semaphore("sem_b"))
        sem_cc = E(nc.semaphore("sem_cc"))
        sem_r = E(nc.semaphore("sem_r"))
        b_outbox = [Buf(), Buf()]
        b_gath = [Buf(), Buf()]

        A0, B0 = 0, 16384

        def aview(lo, n, dt=BF16):
            ap = arena[:, lo:lo + n]
            return ap.bitcast(F32) if dt == F32 else ap

        qT = aview(A0 + 0, 2048)
        kdec = aview(A0 + 2048, 2048)
        vtm = aview(A0 + 4096, 4096)
        Ef = aview(A0 + 8192, 4096, F32)
        laf = aview(A0 + 12288, 4096, F32)
        oT = aview(A0 + 8192, 8192, F32)
        ybuf = aview(A0, 16384, F32)
        rs = aview(B0 + 0, 4096)
        gin = aview(B0 + 4096, 4096)
        cvb = aview(B0 + 8192, 4096)
        mbuf = aview(B0 + 12288, 8192)
        actb = aview(B0, FC * T)

        def ab(lo, n):
            return P.abuf(lo, lo + n)

        b_q = [ab(A0 + h * 512, 512) for h in range(4)]
        b_kdec = ab(A0 + 2048, 2048)
        b_v = ab(A0 + 4096, 4096)
        b_E = ab(A0 + 8192, 4096)
        b_la = [ab(A0 + 12288 + tb * 1024, 1024) for tb in range(4)]
        b_o = [ab(A0 + 8192 + c * 1024, 1024) for c in range(8)]
        b_y = [ab(A0 + c * 1024, 1024) for c in range(16)]
        b_rs = [ab(B0 + c * 512, 512) for c in range(8)]
        b_gin = [ab(B0 + 4096 + c * 512, 512) for c in range(8)]
        b_cvb = [ab(B0 + 8192 + c * 512, 512) for c in range(8)]
        b_m = [ab(B0 + 12288 + c * 512, 512) for c in range(16)]
        b_act = [ab(B0 + c * 512, 512) for c in range(FC)]

        b_x = [Buf() for _ in range(KC)]
        b_h = [Buf() for _ in range(KC)]
        b_slot = [Buf() for _ in range(NSLOT)]
        b_sq = [Buf(), Buf()]
        b_rstd = [Buf(), Buf()]
        b_tmp = [Buf() for _ in range(4)]
        b_ubuf = [Buf(), Buf()]
        b_cbuf = [Buf(), Buf()]
        b_S = [Buf() for _ in range(4)]
        b_Sb = [Buf(), Buf()]
        b_lrT = Buf()
        b_const = Buf()
        b_mod = Buf()
        b_coef = Buf()
        b_decay = Buf()
        b_uh = [Buf() for _ in range(8)]
        b_gh = [Buf() for _ in range(FC)]

        def xc(c):
            return xres[:, c * T:(c + 1) * T]

        def hc(c):
            return hbuf[:, c * T:(c + 1) * T]

        def slot(i, n=SLOTW):
            return ring[:, i * SLOTW:i * SLOTW + n]

        piece_ctr = [0]

        def load_piece(src_ap, ncols):
            i = piece_ctr[0]
            piece_ctr[0] += 1
            s = i % NSLOT
            P.dma("pool", lambda e, s=s, src_ap=src_ap, ncols=ncols: e.dma_start(out=slot(s, ncols), in_=src_ap),
                  slot_sem[s], writes=[b_slot[s]])
            return s

        def cdma(dst, src, eng="sp", sem=None):
            return P.dma(eng, lambda e: e.dma_start(out=dst, in_=src), sem or sem_c, writes=[b_const])

        cdma(cst[:], cvec)
        cdma(badat[:], bada)
        cdma(ngt[:], ngd)
        cdma(glagt[:], glag)
        cdma(cmwt[:], cmw)
        cdma(fcwt[:], fcw)
        cdma(maskU[:], masku)
        cdma(cindt[:], cind)
        cdma(flagt[:], flags)
        P.dma("sp", lambda e: e.dma_start(out=S32[:], in_=s_in), sem_c, writes=b_S + [b_const])
        P.dma("sp", lambda e: e.dma_start(out=uhalo[:], in_=uh_in), sem_c, writes=b_uh + [b_const])
        P.dma("sp", lambda e: e.dma_start(out=ghalo[:], in_=gh_in), sem_c, writes=b_gh + [b_const])
        P.dma("pool", lambda e: e.dma_start(out=wlrt[:], in_=wlr), sem_c, writes=[b_const])
        P.dma("pool", lambda e: e.dma_start(out=wa2t[0:17, :], in_=wa2e), sem_c, writes=[b_const])

        P.op("dve", lambda e: e.memset(onesD[:], 1.0 / D), writes=[b_const])
        P.op("dve", lambda e: e.memset(ones256[:], 1.0 / 256), writes=[b_const])
        P.op("dve", lambda e: e.memset(lrT[:], 1.0), writes=[b_lrT])
        P.op("dve", lambda e: e.memset(epst[:], EPS), writes=[b_const])

        P.op("act", lambda e: e.activation(out=csb[:], in_=cst[:], func=AF.Silu), reads=[b_const], writes=[b_mod])
        MISC = 5
        b_modp = [Buf() for _ in range(4)]
        b_coefp = [Buf() for _ in range(4)]

        def mod_part(part):
            lo, hi = [(0, 32), (32, 48), (48, 80), (80, 96)][part]
            for j in range(lo, hi):
                s_ = load_piece(wada[j], 2048)
                fns = []
                for kc in range(KC):
                    fns.append(lambda e, s_=s_, kc=kc, j=j: e.matmul(
                        banks[MISC][:, j:j + 1], slot(s_)[:, kc * 128:(kc + 1) * 128], csb[:, kc:kc + 1],
                        start=(kc == 0), stop=(kc == KC - 1)))
                P.op("pe", fns, reads=[b_slot[s_], b_mod], writes=[bankb[MISC]])
            P.op("dve", lambda e: e.tensor_tensor(out=modv[:, lo:hi], in0=banks[MISC][:, lo:hi], in1=badat[:, lo:hi], op=ALU.add),
                 reads=[bankb[MISC], b_const], writes=[b_modp[part]])
            bm, bc = b_modp[part], b_coefp[part]
            if part == 0:
                P.op("dve", lambda e: e.tensor_scalar(out=coef[:, 0:16], in0=modv[:, 16:32], scalar1=1.0, scalar2=None, op0=ALU.add),
                     reads=[bm], writes=[bc])
                P.op("dve", lambda e: e.tensor_tensor(out=coef[:, 0:16], in0=coef[:, 0:16], in1=ngt[:, 0:16], op=ALU.mult),
                     reads=[bc, b_const], writes=[bc])
                P.op("dve", lambda e: e.tensor_copy(out=coef[:, 16:32], in_=modv[:, 0:16]), reads=[bm, bc], writes=[bc])
            elif part == 1:
                P.op("dve", lambda e: e.tensor_tensor(out=coef[:, 32:48], in0=modv[:, 32:48], in1=ngt[:, 16:32], op=ALU.mult),
                     reads=[bm, b_const], writes=[bc])
            elif part == 2:
                P.op("dve", lambda e: e.tensor_scalar(out=coef[:, 48:64], in0=modv[:, 64:80], scalar1=1.0, scalar2=None, op0=ALU.add),
                     reads=[bm], writes=[bc])
                P.op("dve", lambda e: e.tensor_tensor(out=coef[:, 48:64], in0=coef[:, 48:64], in1=ngt[:, 32:48], op=ALU.mult),
                     reads=[bc, b_const], writes=[bc])
                P.op("dve", lambda e: e.tensor_copy(out=coef[:, 64:80], in_=modv[:, 48:64]), reads=[bm, bc], writes=[bc])
            else:
                P.op("dve", lambda e: e.tensor_tensor(out=coef[:, 80:96], in0=modv[:, 80:96], in1=ngt[:, 48:64], op=ALU.mult),
                     reads=[bm, b_const], writes=[bc])

        mod_part(0)

        mmring = [0]

        def next_bank():
            b = mmring[0] % 4
            mmring[0] += 1
            return b

        STAT = 4

        def norm_in(cofs, b_coef, first_kc=None):
            for c in range(KC):
                k = c % 2
                P.op("act", lambda e, c=c, k=k: e.activation(out=sq[:, k * T:(k + 1) * T], in_=xc(c), func=AF.Square),
                     reads=[b_x[c]], writes=[b_sq[k]])
                P.op("pe", lambda e, c=c, k=k: e.matmul(banks[STAT][:], onesD[:], sq[:, k * T:(k + 1) * T],
                                                          start=(c == 0), stop=(c == KC - 1)),
                     reads=[b_sq[k], b_const], writes=[bankb[STAT]])
            P.op("act", lambda e: e.activation(out=rstd[:, 0:T], in_=banks[STAT][:], func=AF.Sqrt, bias=epst[:, 0:1], scale=1.0),
                 reads=[bankb[STAT], b_const], writes=[b_rstd[0]])
            P.op("dve", lambda e: e.reciprocal(out=rstd[:, 0:T], in_=rstd[:, 0:T]), reads=[b_rstd[0]], writes=[b_rstd[0]])
            for c in range(KC):
                k = c % 2
                P.op("dve", lambda e, c=c, k=k: e.scalar_tensor_tensor(
                    out=tmpf[:, k * T:(k + 1) * T], in0=xc(c), scalar=coef[:, cofs + c:cofs + c + 1], in1=rstd[:, 0:T],
                    op0=ALU.mult, op1=ALU.mult), reads=[b_x[c], b_rstd[0], b_coef], writes=[b_tmp[k]])
                P.op("act", lambda e, c=c, k=k: e.activation(
                    out=hc(c), in_=tmpf[:, k * T:(k + 1) * T], func=AF.Identity,
                    bias=coef[:, cofs + 16 + c:cofs + 17 + c], scale=1.0), reads=[b_tmp[k], b_coef], writes=[b_h[c]])

        def mm_std(src_ap, rhs_of, nk, rhs_bufs, per_kc=False):
            s = load_piece(src_ap, nk * 128)
            b = next_bank()
            fns = []
            for kc in range(nk):
                fns.append(lambda e, s=s, kc=kc, b=b: e.matmul(
                    banks[b][:], slot(s)[:, kc * 128:(kc + 1) * 128], rhs_of(kc), start=(kc == 0), stop=(kc == nk - 1)))
            if per_kc:
                for kc in range(nk):
                    P.op("pe", fns[kc], reads=[b_slot[s], rhs_bufs[kc]], writes=[bankb[b]])
            else:
                P.op("pe", fns, reads=[b_slot[s]] + rhs_bufs, writes=[bankb[b]])
            return b

        def residual(gofs, b_coef):
            P.op("act", lambda e: e.activation(out=rstd[:, T:2 * T], in_=banks[STAT][:], func=AF.Sqrt, bias=epst[:, 0:1], scale=1.0),
                 reads=[bankb[STAT], b_const], writes=[b_rstd[1]])
            P.op("dve", lambda e: e.reciprocal(out=rstd[:, T:2 * T], in_=rstd[:, T:2 * T]), reads=[b_rstd[1]], writes=[b_rstd[1]])
            for c in range(KC):
                k = 2 + c % 2
                P.op("dve", lambda e, c=c, k=k: e.scalar_tensor_tensor(
                    out=tmpf[:, k * T:(k + 1) * T], in0=ybuf[:, c * T:(c + 1) * T], scalar=coef[:, gofs + c:gofs + c + 1],
                    in1=rstd[:, T:2 * T], op0=ALU.mult, op1=ALU.mult), reads=[b_y[c], b_rstd[1], b_coef], writes=[b_tmp[k]])
                P.op("dve", lambda e, c=c, k=k: e.tensor_tensor(out=xc(c), in0=xc(c), in1=tmpf[:, k * T:(k + 1) * T], op=ALU.add),
                     reads=[b_tmp[k], b_x[c]], writes=[b_x[c]])

        def out_proj_to_y(j, b):
            k = j % 2
            P.op("act", lambda e, j=j, b=b: e.activation(out=ybuf[:, j * T:(j + 1) * T], in_=banks[b][:], func=AF.Identity),
                 reads=[bankb[b]], writes=[b_y[j]])
            P.op("act", lambda e, b=b, k=k: e.activation(out=sq[:, k * T:(k + 1) * T], in_=banks[b][:], func=AF.Square),
                 reads=[bankb[b]], writes=[b_sq[k]])
            pending_stat.append((j, k))

        pending_stat = []

        def flush_stat(all_=False):
            while pending_stat and (all_ or len(pending_stat) > 1):
                j, k = pending_stat.pop(0)
                P.op("pe", lambda e, j=j, k=k: e.matmul(banks[STAT][:], onesD[:], sq[:, k * T:(k + 1) * T],
                                                          start=(j == 0), stop=(j == KC - 1)),
                     reads=[b_sq[k], b_const], writes=[bankb[STAT]])

        def dbg(name, ap, bufs, eng="sp"):
            if debug:
                P.dma(eng, lambda e: e.dma_start(out=dbg_out[name], in_=ap), sem_o, reads=bufs)

        for step in range(n_steps):
            for qd in range(4):
                P.dma("sp", lambda e, qd=qd, step=step: e.dma_start(
                    out=xres[:, qd * 2048:(qd + 1) * 2048], in_=xin[step][:, qd * 2048:(qd + 1) * 2048]),
                    sem_x, writes=b_x[qd * 4:(qd + 1) * 4])

            if fused and step >= 1:
                for hv in range(2):
                    P.dma("sp", lambda e, hv=hv: e.dma_start(out=ybuf[:, hv * 4096:(hv + 1) * 4096], in_=gath[hv][0:128, :]),
                          sem_r, reads=[b_gath[hv]], writes=b_y[hv * 8:(hv + 1) * 8])
                for c in range(KC):
                    P.op("dve", lambda e, c=c: e.scalar_tensor_tensor(
                        out=xc(c), in0=ybuf[:, c * T:(c + 1) * T], scalar=flagt[:, 0:1], in1=xc(c), op0=ALU.mult, op1=ALU.add),
                        reads=[b_y[c], b_x[c], b_const], writes=[b_x[c]])

            norm_in(0, b_coefp[0])
            dbg("d_h", hbuf[:].bitcast(F32), b_h)
            if step == 0:
                dbg("d_mod", modv[:], b_modp)

            for kc in range(KC):
                P.op("pe", lambda e, kc=kc: e.matmul(banks[MISC][0:16, :], wlrt[:, kc * 16:(kc + 1) * 16], hc(kc),
                                                      start=(kc == 0), stop=(kc == KC - 1)),
                     reads=[b_h[kc], b_const], writes=[bankb[MISC]])
            P.op("act", lambda e: e.activation(out=lrT[0:16, :], in_=banks[MISC][0:16, :], func=AF.Identity),
                 reads=[bankb[MISC]], writes=[b_lrT])

            def mm_tok(piece):
                s = load_piece(win[piece], 2048)
                b = next_bank()
                fns = []
                for tb in range(4):
                    for kc in range(KC):
                        fns.append(lambda e, s=s, b=b, tb=tb, kc=kc: e.matmul(
                            banks[b][:, tb * 128:(tb + 1) * 128], hbuf[:, kc * T + tb * 128: kc * T + (tb + 1) * 128],
                            slot(s)[:, kc * 128:(kc + 1) * 128], start=(kc == 0), stop=(kc == KC - 1)))
                P.op("pe", fns, reads=[b_slot[s]] + b_h, writes=[bankb[b]])
                return b

            kd3 = kdec.rearrange("p (tb c) -> p tb c", tb=4)
            E3 = Ef.rearrange("p (tb c) -> p tb c", tb=4)
            v3 = vtm.rearrange("p (tb c) -> p tb c", tb=4)

            def v_piece(jv):
                b = mm_tok(PV + jv)
                P.op("act", lambda e, b=b, jv=jv: e.activation(
                    out=v3[:, :, jv * 128:(jv + 1) * 128], in_=banks[b][:].rearrange("p (tb c) -> p tb c", tb=4),
                    func=AF.Identity), reads=[bankb[b]], writes=[b_v])

            for jv in range(4):
                v_piece(jv)
            for tb in range(4):
                b = next_bank()
                P.op("pe", lambda e, tb=tb, b=b: e.matmul(banks[b][:], lrT[0:17, tb * 128:(tb + 1) * 128], wa2t[0:17, :],
                                                           start=True, stop=True),
                     reads=[b_lrT, b_const], writes=[bankb[b]])
                k = tb % 2
                P.op("act", lambda e, b=b, k=k: e.activation(out=tmpf[:, k * T:(k + 1) * T], in_=banks[b][:], func=AF.Exp, scale=-1.0),
                     reads=[bankb[b]], writes=[b_tmp[k]])
                P.op("act", lambda e, tb=tb, k=k: e.activation(out=laf[:, tb * T:(tb + 1) * T], in_=tmpf[:, k * T:(k + 1) * T],
                                                                 func=AF.Ln, bias=1.0, scale=1.0),
                     reads=[b_tmp[k]], writes=[b_la[tb]])
            dbg("d_la", laf, b_la)
            for jv in range(4, 8):
                v_piece(jv)
            for tb in range(4):
                b = next_bank()
                P.op("pe", lambda e, tb=tb, b=b: e.matmul(banks[b][:], maskU[:], laf[:, tb * T:(tb + 1) * T], start=True, stop=True),
                     reads=[b_la[tb], b_const], writes=[bankb[b]])
                P.op("act", lambda e, tb=tb, b=b: e.activation(out=Ef[:, tb * T:(tb + 1) * T], in_=banks[b][:], func=AF.Exp, scale=-1.0 / 16),
                     reads=[bankb[b]], writes=[b_E])
            dbg("d_E", Ef, [b_E])
            fns = []
            for tb in range(4):
                for hd in range(4):
                    col = (tb * 4 + hd) * 2
                    fns.append(lambda e, tb=tb, hd=hd, col=col: e.matmul(
                        banks[MISC][:, col:col + 2], laf[:, tb * T + hd * 128: tb * T + (hd + 1) * 128], cindt[:],
                        start=True, stop=True))
            P.op("pe", fns, reads=b_la + [b_const], writes=[bankb[MISC]])
            P.op("act", lambda e: e.activation(out=decay[:], in_=banks[MISC][:, 0:32], func=AF.Exp, scale=-1.0 / 16),
                 reads=[bankb[MISC]], writes=[b_decay])
            dbg("d_dec", decay[:], [b_decay])
            for hd in range(4):
                b = mm_std(win[PQ + hd], hc, KC, b_h)
                P.op("act", lambda e, b=b, hd=hd: e.activation(out=qT[:, hd * T:(hd + 1) * T], in_=banks[b][:], func=AF.Identity,
                                                                 scale=float(128 ** -0.5)), reads=[bankb[b]], writes=[b_q[hd]])
            for hd in range(4):
                b = mm_tok(PK + hd)
                P.op("dve", lambda e, b=b, hd=hd: e.tensor_tensor(
                    out=kd3[:, :, hd * 128:(hd + 1) * 128], in0=banks[b][:].rearrange("p (tb c) -> p tb c", tb=4),
                    in1=E3[:, :, hd * 128:(hd + 1) * 128], op=ALU.mult), reads=[bankb[b], b_E], writes=[b_kdec])
            dbg("d_q", qT.bitcast(F32), b_q)
            dbg("d_k", kdec.bitcast(F32), [b_kdec])
            dbg("d_v", vtm.bitcast(F32), [b_v])
            if step == 0:
                mod_part(1)

            o3 = oT.rearrange("p (c t) -> p c t", c=8)
            UB = [6, 6, 7, 7]
            OB = MISC

            def gla_out(ci):
                kk = ci % 2
                fns = []
                for hd in range(4):
                    for hf in range(2):
                        col = (hd * 2 + hf) * 64
                        fns.append(lambda e, hd=hd, kk=kk, hf=hf, ci=ci, col=col: e.matmul(
                            banks[OB][:, col:col + 64],
                            Sb[:, kk * 1024 + hd * 256 + hf * 128: kk * 1024 + hd * 256 + (hf + 1) * 128],
                            qT[:, hd * T + ci * 64: hd * T + (ci + 1) * 64], start=True, stop=True))
                P.op("pe", fns, reads=[b_Sb[kk]] + b_q, writes=[bankb[OB]])
                P.op("act", lambda e, ci=ci: e.activation(
                    out=o3[:, :, ci * 64:(ci + 1) * 64], in_=banks[OB][:].rearrange("p (c t) -> p c t", c=8), func=AF.Identity),
                    reads=[bankb[OB]], writes=b_o)

            def r_piece(jr):
                b = mm_std(win[PR + jr], hc, KC, b_h)
                P.op("act", lambda e, b=b, jr=jr: e.activation(out=rs[:, jr * T:(jr + 1) * T], in_=banks[b][:], func=AF.Silu),
                     reads=[bankb[b]], writes=[b_rs[jr]])

            for ci in range(8):
                tb, half = ci // 2, ci % 2
                r0 = half * 64
                kk = ci % 2
                for hd in range(4):
                    ub = UB[hd]
                    uc = (hd % 2) * 256
                    P.op("pe", lambda e, hd=hd, tb=tb, r0=r0, ub=ub, uc=uc: e.matmul(
                        banks[ub][:, uc:uc + 256], kdec[r0:r0 + 64, tb * 512 + hd * 128: tb * 512 + (hd + 1) * 128],
                        vtm[r0:r0 + 64, tb * 1024 + hd * 256: tb * 1024 + (hd + 1) * 256], start=True, stop=True),
                        reads=[b_kdec, b_v], writes=[bankb[ub]])
                for hd in range(4):
                    ub = UB[hd]
                    uc = (hd % 2) * 256
                    dcol = (tb * 4 + hd) * 2 + half
                    P.op("dve", lambda e, hd=hd, dcol=dcol, ub=ub, uc=uc: e.scalar_tensor_tensor(
                        out=S32[:, hd * 256:(hd + 1) * 256], in0=S32[:, hd * 256:(hd + 1) * 256], scalar=decay[:, dcol:dcol + 1],
                        in1=banks[ub][:, uc:uc + 256], op0=ALU.mult, op1=ALU.add), reads=[bankb[ub], b_decay, b_S[hd]], writes=[b_S[hd]])
                P.op("act", lambda e, kk=kk: e.activation(out=Sb[:, kk * 1024:(kk + 1) * 1024], in_=S32[:], func=AF.Identity),
                     reads=b_S, writes=[b_Sb[kk]])
                r_piece(ci)
                if ci >= 1:
                    gla_out(ci - 1)
            gla_out(7)
            dbg("d_o", oT, b_o)

            for hd in range(4):
                for hf in range(2):
                    c = hd * 2 + hf
                    k = hf
                    P.op("act", lambda e, c=c, k=k: e.activation(out=sq[:, k * T:(k + 1) * T], in_=oT[:, c * T:(c + 1) * T], func=AF.Square),
                         reads=[b_o[c]], writes=[b_sq[k]])
                    P.op("pe", lambda e, k=k, hf=hf: e.matmul(banks[STAT][:], ones256[:], sq[:, k * T:(k + 1) * T],
                                                                start=(hf == 0), stop=(hf == 1)),
                         reads=[b_sq[k], b_const], writes=[bankb[STAT]])
                P.op("act", lambda e: e.activation(out=rstd[:, T:2 * T], in_=banks[STAT][:], func=AF.Sqrt, bias=epst[:, 0:1], scale=1.0),
                     reads=[bankb[STAT], b_const], writes=[b_rstd[1]])
                P.op("dve", lambda e: e.reciprocal(out=rstd[:, T:2 * T], in_=rstd[:, T:2 * T]), reads=[b_rstd[1]], writes=[b_rstd[1]])
                for hf in range(2):
                    c = hd * 2 + hf
                    k = 2 + hf
                    P.op("dve", lambda e, c=c, k=k, hf=hf: e.scalar_tensor_tensor(
                        out=tmpf[:, k * T:(k + 1) * T], in0=oT[:, c * T:(c + 1) * T], scalar=glagt[:, hf:hf + 1],
                        in1=rstd[:, T:2 * T], op0=ALU.mult, op1=ALU.mult), reads=[b_o[c], b_rstd[1], b_const], writes=[b_tmp[k]])
                    P.op("dve", lambda e, c=c, k=k: e.tensor_tensor(
                        out=gin[:, c * T:(c + 1) * T], in0=tmpf[:, k * T:(k + 1) * T], in1=rs[:, c * T:(c + 1) * T], op=ALU.mult),
                        reads=[b_tmp[k], b_rs[c]], writes=[b_gin[c]])
            dbg("d_gin", gin.bitcast(F32), b_gin)

            if step == 0:
                mod_part(2)
            for c in range(8):
                k = c % 2
                ub = ubuf[:, k * (T + 2):(k + 1) * (T + 2)]
                cb_ = cbuf[:, k * T:(k + 1) * T]
                b1 = mm_std(win[PCC + c], hc, KC, b_h)
                P.op("act", lambda e, b1=b1, k=k: e.activation(out=tmpf[:, k * T:(k + 1) * T], in_=banks[b1][:], func=AF.Identity),
                     reads=[bankb[b1]], writes=[b_tmp[k]])
                b2 = mm_std(win[PCX + c], hc, KC, b_h)
                P.op("dve", lambda e, ub=ub, c=c: e.tensor_copy(out=ub[:, 0:2], in_=uhalo[:, 2 * c:2 * c + 2]),
                     reads=[b_uh[c]], writes=[b_ubuf[k]])
                P.op("dve", lambda e, ub=ub, b2=b2, k=k: e.tensor_tensor(out=ub[:, 2:T + 2], in0=banks[b2][:], in1=tmpf[:, k * T:(k + 1) * T], op=ALU.mult),
                     reads=[bankb[b2], b_tmp[k], b_ubuf[k]], writes=[b_ubuf[k]])
                P.op("dve", lambda e, ub=ub, c=c: e.tensor_copy(out=uhalo[:, 2 * c:2 * c + 2], in_=ub[:, T:T + 2]),
                     reads=[b_ubuf[k]], writes=[b_uh[c]])
                P.op("dve", lambda e, ub=ub, cb_=cb_, c=c: e.tensor_scalar(out=cb_, in0=ub[:, 0:T], scalar1=cmwt[:, 3 * c:3 * c + 1], scalar2=None, op0=ALU.mult),
                     reads=[b_ubuf[k], b_const], writes=[b_cbuf[k]])
                P.op("dve", lambda e, ub=ub, cb_=cb_, c=c: e.scalar_tensor_tensor(out=cb_, in0=ub[:, 1:T + 1], scalar=cmwt[:, 3 * c + 1:3 * c + 2], in1=cb_, op0=ALU.mult, op1=ALU.add),
                     reads=[b_ubuf[k], b_cbuf[k]], writes=[b_cbuf[k]])
                P.op("dve", lambda e, ub=ub, cb_=cb_, c=c: e.scalar_tensor_tensor(out=cb_, in0=ub[:, 2:T + 2], scalar=cmwt[:, 3 * c + 2:3 * c + 3], in1=cb_, op0=ALU.mult, op1=ALU.add),
                     reads=[b_ubuf[k], b_cbuf[k]], writes=[b_cbuf[k]])
                b3 = mm_std(win[PCB + c], hc, KC, b_h)
                P.op("dve", lambda e, b3=b3, cb_=cb_, c=c: e.tensor_tensor(out=cvb[:, c * T:(c + 1) * T], in0=banks[b3][:], in1=cb_, op=ALU.mult),
                     reads=[bankb[b3], b_cbuf[k]], writes=[b_cvb[c]])
            dbg("d_cvb", cvb.bitcast(F32), b_cvb)

            for j in range(16):
                ka, kb = (j % 2) * 2, (j % 2) * 2 + 1
                ta = tmpf[:, ka * T:(ka + 1) * T]
                tbv = tmpf[:, kb * T:(kb + 1) * T]
                b1 = mm_std(win[PGA + j], hc, KC, b_h)
                P.op("act", lambda e, b1=b1, ta=ta: e.activation(out=ta, in_=banks[b1][:], func=AF.Sigmoid),
                     reads=[bankb[b1]], writes=[b_tmp[ka]])
                b2 = mm_std(woutg[j], lambda kc: gin[:, kc * T:(kc + 1) * T], 8, b_gin)
                P.op("dve", lambda e, b2=b2, ta=ta: e.tensor_tensor(out=ta, in0=banks[b2][:], in1=ta, op=ALU.mult),
                     reads=[bankb[b2], b_tmp[ka]], writes=[b_tmp[ka]])
                b3 = mm_std(win[PGB + j], hc, KC, b_h)
                P.op("act", lambda e, b3=b3, tbv=tbv: e.activation(out=tbv, in_=banks[b3][:], func=AF.Sigmoid),
                     reads=[bankb[b3]], writes=[b_tmp[kb]])
                b4 = mm_std(woutc[j], lambda kc: cvb[:, kc * T:(kc + 1) * T], 8, b_cvb)
                P.op("dve", lambda e, b4=b4, tbv=tbv: e.tensor_tensor(out=tbv, in0=banks[b4][:], in1=tbv, op=ALU.mult),
                     reads=[bankb[b4], b_tmp[kb]], writes=[b_tmp[kb]])
                P.op("dve", lambda e, j=j, ta=ta, tbv=tbv: e.tensor_tensor(out=mbuf[:, j * T:(j + 1) * T], in0=ta, in1=tbv, op=ALU.add),
                     reads=[b_tmp[ka], b_tmp[kb]], writes=[b_m[j]])
            dbg("d_m", mbuf.bitcast(F32), b_m)

            for j in range(16):
                b = mm_std(wo[j], lambda kc: mbuf[:, kc * T:(kc + 1) * T], KC, b_m)
                flush_stat()
                out_proj_to_y(j, b)
            flush_stat(True)
            residual(32, b_coefp[1])
            dbg("d_xmid", xres[:], b_x)

            norm_in(48, b_coefp[2])
            for j in range(FC):
                k = j % 2
                ub = ubuf[:, k * (T + 2):(k + 1) * (T + 2)]
                cb_ = cbuf[:, k * T:(k + 1) * T]
                b1 = mm_std(wup[j], hc, KC, b_h, per_kc=(j == 0))
                if step == 0 and j == 4:
                    mod_part(3)
                P.op("dve", lambda e, ub=ub, j=j: e.tensor_copy(out=ub[:, 0:2], in_=ghalo[:, 2 * j:2 * j + 2]),
                     reads=[b_gh[j]], writes=[b_ubuf[k]])
                P.op("act", lambda e, ub=ub, b1=b1: e.activation(out=ub[:, 2:T + 2], in_=banks[b1][:], func=AF.Identity),
                     reads=[bankb[b1], b_ubuf[k]], writes=[b_ubuf[k]])
                P.op("dve", lambda e, ub=ub, j=j: e.tensor_copy(out=ghalo[:, 2 * j:2 * j + 2], in_=ub[:, T:T + 2]),
                     reads=[b_ubuf[k]], writes=[b_gh[j]])
                P.op("dve", lambda e, ub=ub, cb_=cb_, j=j: e.tensor_scalar(out=cb_, in0=ub[:, 0:T], scalar1=fcwt[:, 3 * j:3 * j + 1], scalar2=None, op0=ALU.mult),
                     reads=[b_ubuf[k], b_const], writes=[b_cbuf[k]])
                P.op("dve", lambda e, ub=ub, cb_=cb_, j=j: e.scalar_tensor_tensor(out=cb_, in0=ub[:, 1:T + 1], scalar=fcwt[:, 3 * j + 1:3 * j + 2], in1=cb_, op0=ALU.mult, op1=ALU.add),
                     reads=[b_ubuf[k], b_cbuf[k]], writes=[b_cbuf[k]])
                P.op("dve", lambda e, ub=ub, cb_=cb_, j=j: e.scalar_tensor_tensor(out=cb_, in0=ub[:, 2:T + 2], scalar=fcwt[:, 3 * j + 2:3 * j + 3], in1=cb_, op0=ALU.mult, op1=ALU.add),
                     reads=[b_ubuf[k], b_cbuf[k]], writes=[b_cbuf[k]])
                P.op("act", lambda e, cb_=cb_, k=k: e.activation(out=tmpf[:, k * T:(k + 1) * T], in_=cb_, func=AF.Gelu),
                     reads=[b_cbuf[k]], writes=[b_tmp[k]])
                b2 = mm_std(wup[FC + j], hc, KC, b_h)
                P.op("dve", lambda e, b2=b2, j=j, k=k: e.tensor_tensor(out=actb[:, j * T:(j + 1) * T], in0=banks[b2][:], in1=tmpf[:, k * T:(k + 1) * T], op=ALU.mult),
                     reads=[bankb[b2], b_tmp[k]], writes=[b_act[j]])
            for j in range(16):
                b = next_bank()
                for g in range(4):
                    s = load_piece(wdn[j * 4 + g], 1408)
                    fns = []
                    for kk in range(11):
                        kc = g * 11 + kk
                        fns.append(lambda e, s=s, kk=kk, kc=kc, b=b: e.matmul(
                            banks[b][:], slot(s)[:, kk * 128:(kk + 1) * 128], actb[:, kc * T:(kc + 1) * T],
                            start=(kc == 0), stop=(kc == FC - 1)))
                    P.op("pe", fns, reads=[b_slot[s]] + b_act[g * 11:(g + 1) * 11], writes=[bankb[b]])
                flush_stat()
                out_proj_to_y(j, b)
            flush_stat(True)
            residual(80, b_coefp[3])

            if fused and step < n_steps - 1:
                for hv in range(2):
                    P.dma("sp", lambda e, hv=hv: e.dma_start(out=outbox[hv][:, :], in_=xres[:, hv * 4096:(hv + 1) * 4096]),
                          sem_b, reads=b_x[hv * 8:(hv + 1) * 8], writes=[b_outbox[hv]])
                prev = None
                for hv in range(2):
                    prev = P.dma("pool", lambda e, hv=hv: e.collective_compute(
                        "AllGather", ALU.bypass, replica_groups=[[0, 1], [2, 3], [4, 5], [6, 7]],
                        ins=[outbox[hv].ap().opt()], outs=[gath[hv].ap().opt()]),
                        sem_cc, reads=[b_outbox[hv]], writes=[b_gath[hv]], deps=[prev], inc=1)
            for qd in range(4):
                P.dma("sp", lambda e, qd=qd, step=step: e.dma_start(
                    out=xout[step][:, qd * 2048:(qd + 1) * 2048], in_=xres[:, qd * 2048:(qd + 1) * 2048]),
                    sem_o, reads=b_x[qd * 4:(qd + 1) * 4])
            if fused and step == 0:
                P.op("dve", lambda e: e.tensor_scalar(out=S32[:], in0=S32[:], scalar1=flagt[:, 1:2], scalar2=None, op0=ALU.mult),
                     reads=b_S + [b_const], writes=b_S)
                P.op("dve", lambda e: e.tensor_scalar(out=uhalo[:], in0=uhalo[:], scalar1=flagt[:, 1:2], scalar2=None, op0=ALU.mult),
                     reads=b_uh + [b_const], writes=b_uh)
                P.op("dve", lambda e: e.tensor_scalar(out=ghalo[:], in0=ghalo[:], scalar1=flagt[:, 1:2], scalar2=None, op0=ALU.mult),
                     reads=b_gh + [b_const], writes=b_gh)

        P.dma("sp", lambda e: e.dma_start(out=s_out, in_=S32[:]), sem_o, reads=b_S)
        P.dma("sp", lambda e: e.dma_start(out=uh_out, in_=uhalo[:]), sem_o, reads=b_uh)
        last = P.dma("sp", lambda e: e.dma_start(out=gh_out, in_=ghalo[:]), sem_o, reads=b_gh)

        print("sbuf bytes remaining", nc.sbuf_bytes_remaining)
        P.finalize(engsem)
        final_o = P.dma_counts[id(sem_o)]

        block = E(nc.Block())

        @block.tensor
        def _(e):
            P.replay("pe", e)

        @block.scalar
        def _(e):
            P.replay("act", e)

        @block.vector
        def _(e):
            P.replay("dve", e)

        @block.gpsimd
        def _(e):
            P.replay("pool", e)

        @block.sync
        def _(e):
            P.replay("sp", e)
            e.wait_ge(sem_o, final_o)

    return nc


def _pieces(w, col0s, nk):
    out = np.empty((len(col0s), 128, nk * 128), np.float32)
    wk = w.reshape(nk, 128, w.shape[1])
    for j, c0 in enumerate(col0s):
        out[j] = wk[:, :, c0:c0 + 128].transpose(1, 0, 2).reshape(128, nk * 128)
    return out


def _fm(v):
    return np.ascontiguousarray(v.reshape(-1, 128).T)


_MASKU = None


def _consts():
    global _MASKU
    if _MASKU is None:
        j = np.arange(128)[:, None]
        l = np.arange(128)[None, :]
        mu = ((j > l) & ((j // 64) == (l // 64))).astype(np.float32)
        ci = np.zeros((128, 2), np.float32)
        ci[:64, 0] = 1.0
        ci[64:, 1] = 1.0
        _MASKU = (mu, ci)
    return _MASKU


def layer_maps(l, w_ada, b_ada, norm_g, w_in, w_a2, b_a2, gla_norm_g, w_out_gla, conv_mix_w,
               w_out_conv, w_o, w_up, ffn_conv_w, w_down):
    col0 = [j * 128 for j in range(24)] + [3088 + j * 128 for j in range(56)]
    mu, ci = _consts()
    m = {}
    m["wada"] = _pieces(w_ada[l], [j * 128 for j in range(96)], 16)
    m["bada"] = _fm(b_ada[l])
    m["ng"] = np.ascontiguousarray(np.concatenate([_fm(norm_g[l, i]) for i in range(4)], axis=1))
    m["win"] = _pieces(w_in[l], col0, 16)
    m["wlr"] = np.ascontiguousarray(w_in[l][:, 3072:3088].reshape(16, 128, 16).transpose(1, 0, 2).reshape(128, 256))
    m["wa2e"] = np.ascontiguousarray(np.concatenate([w_a2[l], b_a2[l][None, :]], axis=0))
    m["glag"] = _fm(gla_norm_g[l])
    m["woutg"] = _pieces(w_out_gla[l], [j * 128 for j in range(16)], 8)
    m["woutc"] = _pieces(w_out_conv[l], [j * 128 for j in range(16)], 8)
    m["cmw"] = np.ascontiguousarray(conv_mix_w[l].reshape(3, 8, 128).transpose(2, 1, 0).reshape(128, 24))
    m["wo"] = _pieces(w_o[l], [j * 128 for j in range(16)], 16)
    m["wup"] = _pieces(w_up[l], [j * 128 for j in range(88)], 16)
    m["fcw"] = np.ascontiguousarray(ffn_conv_w[l].reshape(3, FC, 128).transpose(2, 1, 0).reshape(128, 132))
    wd = _pieces(w_down[l], [j * 128 for j in range(16)], FC)
    m["wdn"] = np.ascontiguousarray(wd.reshape(16, 128, 4, 1408).transpose(0, 2, 1, 3).reshape(64, 128, 1408))
    m["masku"] = mu
    m["cind"] = ci
    return m


def x_tiles(xb):
    s = xb.shape[0]
    return np.ascontiguousarray(xb.reshape(s // T, T, KC, 128).transpose(0, 3, 2, 1).reshape(s // T, 128, KC * T))


def x_untile(t):
    n = t.shape[0]
    return np.ascontiguousarray(t.reshape(n, 128, KC, T).transpose(0, 3, 2, 1).reshape(n * T, D))


_NC_CACHE = {}


def _get_nc():
    if "nc" not in _NC_CACHE:
        _NC_CACHE["nc"] = build(NTILE + 1, fused=True)
    return _NC_CACHE["nc"]


def kernel(x, c, w_ada, b_ada, norm_g, w_in, w_a2, b_a2, gla_norm_g, w_out_gla,
           conv_mix_w, w_out_conv, w_o, w_up, ffn_conv_w, w_down):
    args = [np.asarray(a, np.float32) for a in (w_ada, b_ada, norm_g, w_in, w_a2, b_a2, gla_norm_g, w_out_gla,
                                                conv_mix_w, w_out_conv, w_o, w_up, ffn_conv_w, w_down)]
    x = np.asarray(x, np.float32)
    c = np.asarray(c, np.float32)
    B = x.shape[0]
    nc = _get_nc()
    lm = [layer_maps(l, *args) for l in range(2)]
    zeros_x = np.zeros((NTILE + 1, 128, KC * T), np.float32)
    fl = [np.tile(np.array([[0.0, 1.0]], np.float32), (128, 1)), np.tile(np.array([[1.0, 0.0]], np.float32), (128, 1))]
    in_maps = []
    for core in range(2 * B):
        b, l = core // 2, core % 2
        m = dict(lm[l])
        m["cvec"] = _fm(c[b])
        if l == 0:
            m["xin"] = np.concatenate([x_tiles(x[b]), zeros_x[:1]], axis=0)
        else:
            m["xin"] = zeros_x
        m["flags"] = fl[l]
        m["s_in"] = np.zeros((128, 1024), np.float32)
        m["uh_in"] = np.zeros((128, 16), np.float32)
        m["gh_in"] = np.zeros((128, 88), np.float32)
        in_maps.append(m)
    res = run_bass_kernel_spmd(nc, in_maps, core_ids=list(range(2 * B)))
    out = np.stack([x_untile(np.asarray(res.results[2 * b + 1]["xout"])[1:]) for b in range(B)], axis=0)
    return out.astype(np.float32)
```
